# Optimizing a Trainium2 kernel written in Bass

```python
import jax
import jax.numpy as jnp
from jax import lax
import numpy as np

D_MODEL = 1024
BATCH = 8
SEQ = 4096
DEPTH = 4

GRID_W = 64
CTX_LEN = 256
N_MIXERS = 4
HEAD_DIM = 64
GROUP_WIDTH = D_MODEL // N_MIXERS
GROUP_HEADS = GROUP_WIDTH // HEAD_DIM

NA_HEADS = GROUP_HEADS
NA_WIN_ROWS = 8
NA_WIN_COLS = 16

GQA_HEADS = GROUP_HEADS
GQA_KV_HEADS = GROUP_HEADS // 2

MLA_HEADS = GROUP_HEADS
MLA_Q_RANK = D_MODEL // 4
MLA_KV_RANK = D_MODEL // 8
MLA_NOPE_DIM = HEAD_DIM
MLA_ROPE_DIM = HEAD_DIM // 2
MLA_V_DIM = HEAD_DIM

RWKV_HEADS = GROUP_HEADS
RWKV_HEAD_DIM = HEAD_DIM
RWKV_WIDTH = RWKV_HEADS * RWKV_HEAD_DIM
RWKV_DECAY_RANK = 64
RWKV_ICLR_RANK = 64
RWKV_GATE_RANK = 160
SHIFT_TAPS = 3

MLP_HIDDEN = 4 * D_MODEL

NA_COLS = 3 * NA_HEADS * HEAD_DIM
GQA_COLS = (GQA_HEADS + 2 * GQA_KV_HEADS) * HEAD_DIM
MLA_COLS = MLA_Q_RANK + MLA_KV_RANK + MLA_ROPE_DIM
RWKV_COLS = 3 * RWKV_WIDTH + 2 * RWKV_DECAY_RANK + 2 * RWKV_ICLR_RANK + RWKV_GATE_RANK
IN_COLS = NA_COLS + GQA_COLS + MLA_COLS + RWKV_COLS
MIX_WIDTH = NA_HEADS * HEAD_DIM + GQA_HEADS * HEAD_DIM + MLA_HEADS * MLA_V_DIM + RWKV_WIDTH

Q_BLOCK = 128
ROPE_THETA = 10000.0
RMS_EPS = 1e-6
GN_EPS = 64e-5
NEG_INF = -1e30
F32 = jnp.float32

kernel_name = 'hybrid_na_gqa_mla_rwkv7_dit'


def rms_norm(x, g, eps=RMS_EPS):
    xf = x.astype(F32)
    y = xf * lax.rsqrt(jnp.mean(xf * xf, axis=-1, keepdims=True) + eps)
    return (y * g.astype(F32)).astype(x.dtype)


def split_cols(u, sizes):
    return jnp.split(u, np.cumsum(sizes)[:-1].tolist(), axis=-1)


def to_heads(u, n_heads):
    return u.reshape(u.shape[:-1] + (n_heads, u.shape[-1] // n_heads))


def axial_rope_tables(n_tokens, rot_dim):
    t = jnp.arange(n_tokens, dtype=jnp.int32)
    rows = (t // GRID_W).astype(F32)
    cols = (t % GRID_W).astype(F32)
    per_axis = rot_dim // 2
    inv_freq = ROPE_THETA ** (-jnp.arange(0, per_axis, 2, dtype=F32) / per_axis)
    ang = jnp.concatenate([rows[:, None] * inv_freq, cols[:, None] * inv_freq], axis=-1)
    return jnp.cos(ang), jnp.sin(ang)


def apply_rope(x, cos, sin):
    c = cos[None, :, None, :]
    s = sin[None, :, None, :]
    xf = x.astype(F32)
    x1, x2 = xf[..., 0::2], xf[..., 1::2]
    y = jnp.stack([x1 * c - x2 * s, x1 * s + x2 * c], axis=-1)
    return y.reshape(x.shape).astype(x.dtype)


def block_attention(q, k, v):
    bsz, n_q, n_heads, dk = q.shape
    n_kv, dv = k.shape[2], v.shape[-1]
    grp = n_heads // n_kv
    scale = dk ** -0.5
    qb = q.reshape(bsz, n_q // Q_BLOCK, Q_BLOCK, n_kv, grp, dk).swapaxes(0, 1)

    def one_block(qi):
        s = jnp.einsum('bqhgd,bkhd->bhgqk', qi, k).astype(F32) * scale
        p = jax.nn.softmax(s, axis=-1).astype(v.dtype)
        return jnp.einsum('bhgqk,bkhd->bqhgd', p, v)

    o = lax.map(one_block, qb)
    return o.swapaxes(0, 1).reshape(bsz, n_q, n_heads * dv)


def neighbourhood_attention(q, k, v, k_ctx, v_ctx, rpb, rows):
    bsz, n_tok, n_heads, d = q.shape
    wr = min(NA_WIN_ROWS, rows)
    r_idx = np.arange(rows)
    key_rows = np.clip(r_idx - wr // 2, 0, rows - wr)[:, None] + np.arange(wr)[None, :]
    c_idx = np.arange(GRID_W)
    col_start = np.clip(c_idx - NA_WIN_COLS // 2, 0, GRID_W - NA_WIN_COLS)
    col_in_win = (c_idx[None, :] >= col_start[:, None]) & (c_idx[None, :] < col_start[:, None] + NA_WIN_COLS)
    dr = key_rows - r_idx[:, None] + NA_WIN_ROWS - 1
    dc = np.clip(c_idx[None, :] - c_idx[:, None] + NA_WIN_COLS - 1, 0, 2 * NA_WIN_COLS - 2)
    bias = rpb[:, dr[:, None, :, None], dc[None, :, None, :]].astype(F32)
    q_grid = q.reshape(bsz, rows, GRID_W, n_heads, d)
    k_strip = k.reshape(bsz, rows, GRID_W, n_heads, d)[:, key_rows]
    v_strip = v.reshape(bsz, rows, GRID_W, n_heads, d)[:, key_rows]
    scale = d ** -0.5
    s_win = jnp.einsum('brqhd,brjkhd->bhrqjk', q_grid, k_strip).astype(F32) * scale + bias[None]
    s_win = jnp.where(col_in_win[:, None, :], s_win, NEG_INF).reshape(bsz, n_heads, rows, GRID_W, wr * GRID_W)
    s_ctx = jnp.einsum('brqhd,bchd->bhrqc', q_grid, k_ctx).astype(F32) * scale
    p = jax.nn.softmax(jnp.concatenate([s_win, s_ctx], axis=-1), axis=-1).astype(v.dtype)
    p_win = p[..., :wr * GRID_W].reshape(bsz, n_heads, rows, GRID_W, wr, GRID_W)
    p_ctx = p[..., wr * GRID_W:]
    out = (jnp.einsum('bhrqjk,brjkhd->brqhd', p_win, v_strip)
           + jnp.einsum('bhrqc,bchd->brqhd', p_ctx, v_ctx))
    return out.reshape(bsz, n_tok, n_heads * d)


def gqa_queries(u, q_norm_g):
    return rms_norm(to_heads(u[..., :GQA_HEADS * HEAD_DIM], GQA_HEADS), q_norm_g)


def gqa_keys_values(u, k_norm_g):
    k, v = split_cols(u[..., GQA_HEADS * HEAD_DIM:], [GQA_KV_HEADS * HEAD_DIM] * 2)
    return rms_norm(to_heads(k, GQA_KV_HEADS), k_norm_g), to_heads(v, GQA_KV_HEADS)


def mla_queries(u, q_norm_g, w_uq):
    q = to_heads(rms_norm(u[..., :MLA_Q_RANK], q_norm_g) @ w_uq, MLA_HEADS)
    return q[..., :MLA_NOPE_DIM], q[..., MLA_NOPE_DIM:]


def mla_keys_values(u, kv_norm_g, w_ukv):
    c_kv = u[..., MLA_Q_RANK:MLA_Q_RANK + MLA_KV_RANK]
    k_rope = u[..., MLA_Q_RANK + MLA_KV_RANK:][:, :, None, :]
    kv = to_heads(rms_norm(c_kv, kv_norm_g) @ w_ukv, MLA_HEADS)
    return kv[..., :MLA_NOPE_DIM], k_rope, kv[..., MLA_NOPE_DIM:]


def mla_join_key(k_nope, k_rope):
    return jnp.concatenate([k_nope, jnp.broadcast_to(k_rope, k_nope.shape[:-1] + (MLA_ROPE_DIM,))], axis=-1)


def centred_token_shift(u, taps):
    up = jnp.pad(u, ((0, 0), (1, 1), (0, 0)))
    return up[:, :-2] * taps[0] + u * taps[1] + up[:, 2:] * taps[2]


def rwkv7_scan(r, decay, k, v, a_vec, b_vec, s0, reverse):
    seq_first = tuple(jnp.swapaxes(t, 0, 1) for t in (r, decay, k, v, a_vec, b_vec))

    def step(s, inp):
        r_t, w_t, k_t, v_t, a_t, b_t = inp
        sa = jnp.einsum('bhvk,bhk->bhv', s, a_t)
        s = s * w_t[:, :, None, :] + sa[..., None] * b_t[:, :, None, :] + v_t[..., None] * k_t[:, :, None, :]
        return s, jnp.einsum('bhvk,bhk->bhv', s, r_t)

    s_final, ys = lax.scan(step, s0, seq_first, reverse=reverse)
    return s_final, jnp.swapaxes(ys, 0, 1)


def rwkv7_time_mix(u, s0, params, want_out):
    w0, w2, a0, a2, k_k, k_a, r_k, g2, lnx_w, lnx_b = params
    uf = u.astype(F32)
    r, k, v, wl_f, wl_b, al_f, al_b, g_low = split_cols(
        uf, [RWKV_WIDTH] * 3 + [RWKV_DECAY_RANK] * 2 + [RWKV_ICLR_RANK] * 2 + [RWKV_GATE_RANK])
    bsz, n_tok = u.shape[0], u.shape[1]
    if s0 is None:
        zero = jnp.zeros((bsz, RWKV_HEADS, RWKV_HEAD_DIM, RWKV_HEAD_DIM), F32)
        s0 = (zero, zero)
    r_h, v_h = to_heads(r, RWKV_HEADS), to_heads(v, RWKV_HEADS)
    y_sum, keys_dir, finals = 0.0, [], []
    for d, (w_low, a_low) in enumerate(((wl_f, al_f), (wl_b, al_b))):
        w = -jax.nn.softplus(-(w0[d] + jnp.tanh(w_low) @ w2[d])) - 0.5
        decay = to_heads(jnp.exp(-jnp.exp(w)), RWKV_HEADS)
        a = jax.nn.sigmoid(a0[d] + a_low @ a2[d])
        kk = to_heads(k * k_k[d], RWKV_HEADS)
        kk = kk / jnp.maximum(jnp.sqrt(jnp.sum(kk * kk, axis=-1, keepdims=True)), 1e-12)
        k_d = to_heads(k * (1.0 + (a - 1.0) * k_a[d]), RWKV_HEADS)
        s_fin, y = rwkv7_scan(r_h, decay, k_d, v_h, -kk, kk * to_heads(a, RWKV_HEADS), s0[d], d == 1)
        y_sum = y_sum + y
        keys_dir.append(k_d)
        finals.append(s_fin)
    states = (finals[0], finals[1])
    if not want_out:
        return None, states
    mu = jnp.mean(y_sum, axis=-1, keepdims=True)
    var = jnp.mean(jnp.square(y_sum - mu), axis=-1, keepdims=True)
    y = ((y_sum - mu) * lax.rsqrt(var + GN_EPS)).reshape(bsz, n_tok, RWKV_WIDTH)
    bonus = (jnp.sum(r_h * keys_dir[0] * r_k[0], axis=-1, keepdims=True)
             + jnp.sum(r_h * keys_dir[1] * r_k[1], axis=-1, keepdims=True)) * v_h
    y = y * lnx_w + lnx_b + bonus.reshape(bsz, n_tok, RWKV_WIDTH)
    gate = jax.nn.sigmoid(g_low) @ g2
    return (y * gate).astype(u.dtype), states


def hybrid_token_mixer(h_lat, h_ctx, rows, w_in, shift_taps, na_rpb, gqa_q_norm, gqa_k_norm,
                       mla_q_norm, mla_kv_norm, mla_w_uq, mla_w_ukv, rwkv_params,
                       rope_head, rope_mla, want_ctx):
    p_lat, p_ctx = h_lat @ w_in, h_ctx @ w_in
    sizes = [NA_COLS, GQA_COLS, MLA_COLS, RWKV_COLS]
    na_l, gqa_l, mla_l, rw_l = split_cols(p_lat, sizes)
    na_c, gqa_c, mla_c, rw_c = split_cols(p_ctx, sizes)
    cos_h, sin_h = rope_head
    cos_m, sin_m = rope_mla

    a_q, a_k, a_v = [to_heads(t, NA_HEADS) for t in split_cols(na_l, [GROUP_WIDTH] * 3)]
    a_qc, a_kc, a_vc = [to_heads(t, NA_HEADS) for t in split_cols(na_c, [GROUP_WIDTH] * 3)]
    out_a = neighbourhood_attention(a_q, a_k, a_v, a_kc, a_vc, na_rpb, rows)

    b_k, b_v = gqa_keys_values(gqa_l, gqa_k_norm)
    b_kc, b_vc = gqa_keys_values(gqa_c, gqa_k_norm)
    b_q = apply_rope(gqa_queries(gqa_l, gqa_q_norm), cos_h, sin_h)
    b_k_all = jnp.concatenate([apply_rope(b_k, cos_h, sin_h), b_kc], axis=1)
    out_b = block_attention(b_q, b_k_all, jnp.concatenate([b_v, b_vc], axis=1))

    c_qn, c_qr = mla_queries(mla_l, mla_q_norm, mla_w_uq)
    c_kn, c_kr, c_v = mla_keys_values(mla_l, mla_kv_norm, mla_w_ukv)
    c_knc, c_krc, c_vc = mla_keys_values(mla_c, mla_kv_norm, mla_w_ukv)
    c_k_ctx = mla_join_key(c_knc, c_krc)
    c_q = jnp.concatenate([c_qn, apply_rope(c_qr, cos_m, sin_m)], axis=-1)
    c_k_all = jnp.concatenate([mla_join_key(c_kn, apply_rope(c_kr, cos_m, sin_m)), c_k_ctx], axis=1)
    out_c = block_attention(c_q, c_k_all, jnp.concatenate([c_v, c_vc], axis=1))

    out_dc, ctx_states = rwkv7_time_mix(centred_token_shift(rw_c, shift_taps), None, rwkv_params, want_ctx)
    out_d, _ = rwkv7_time_mix(centred_token_shift(rw_l, shift_taps), ctx_states, rwkv_params, True)

    mix_lat = jnp.concatenate([out_a, out_b, out_c, out_d], axis=-1)
    if not want_ctx:
        return mix_lat, None
    out_ac = block_attention(a_qc, a_kc, a_vc)
    out_bc = block_attention(gqa_queries(gqa_c, gqa_q_norm), b_kc, b_vc)
    c_qnc, c_qrc = mla_queries(mla_c, mla_q_norm, mla_w_uq)
    out_cc = block_attention(jnp.concatenate([c_qnc, c_qrc], axis=-1), c_k_ctx, c_vc)
    mix_ctx = jnp.concatenate([out_ac, out_bc, out_cc, out_dc], axis=-1)
    return mix_lat, mix_ctx


def squared_relu_mlp(h, w1, w2):
    return jnp.square(jax.nn.relu(h @ w1)) @ w2


def setup_inputs(seed: int = 0) -> dict:
    key = jax.random.key(seed)
    keys = iter(jax.random.split(key, 40))

    def normal(shape, scale):
        return scale * jax.random.normal(next(keys), shape, F32)

    def gain(shape):
        return 1.0 + normal(shape, 0.05)

    shift_base = jnp.array([0.25, 0.5, 0.25], F32)[None, :, None]
    return {
        'x': normal((BATCH, SEQ, D_MODEL), 1.0),
        'c': normal((BATCH, D_MODEL), 1.0),
        'ctx': normal((BATCH, CTX_LEN, D_MODEL), 1.0),
        'c_ctx': normal((D_MODEL,), 1.0),
        'w_mod': normal((DEPTH, D_MODEL, 6 * D_MODEL), 0.5 * D_MODEL ** -0.5),
        'b_mod': normal((DEPTH, 6 * D_MODEL), 0.01),
        'norm1_g': gain((DEPTH, D_MODEL)),
        'norm2_g': gain((DEPTH, D_MODEL)),
        'w_in': normal((DEPTH, D_MODEL, IN_COLS), D_MODEL ** -0.5),
        'rwkv_shift': shift_base + normal((DEPTH, SHIFT_TAPS, RWKV_COLS), 0.05),
        'na_rpb': normal((DEPTH, NA_HEADS, 2 * NA_WIN_ROWS - 1, 2 * NA_WIN_COLS - 1), 0.3),
        'gqa_q_norm': gain((DEPTH, HEAD_DIM)),
        'gqa_k_norm': gain((DEPTH, HEAD_DIM)),
        'mla_q_norm': gain((DEPTH, MLA_Q_RANK)),
        'mla_kv_norm': gain((DEPTH, MLA_KV_RANK)),
        'mla_w_uq': normal((DEPTH, MLA_Q_RANK, MLA_HEADS * (MLA_NOPE_DIM + MLA_ROPE_DIM)), MLA_Q_RANK ** -0.5),
        'mla_w_ukv': normal((DEPTH, MLA_KV_RANK, MLA_HEADS * (MLA_NOPE_DIM + MLA_V_DIM)), MLA_KV_RANK ** -0.5),
        'rwkv_w0': jax.random.uniform(next(keys), (DEPTH, 2, RWKV_WIDTH), F32, -6.0, -1.0),
        'rwkv_w2': normal((DEPTH, 2, RWKV_DECAY_RANK, RWKV_WIDTH), 0.5 * RWKV_DECAY_RANK ** -0.5),
        'rwkv_a0': normal((DEPTH, 2, RWKV_WIDTH), 0.5),
        'rwkv_a2': normal((DEPTH, 2, RWKV_ICLR_RANK, RWKV_WIDTH), 0.5 * RWKV_ICLR_RANK ** -0.5),
        'rwkv_k_k': 0.85 + normal((DEPTH, 2, RWKV_WIDTH), 0.05),
        'rwkv_k_a': gain((DEPTH, 2, RWKV_WIDTH)),
        'rwkv_r_k': normal((DEPTH, 2, RWKV_HEADS, RWKV_HEAD_DIM), 0.1),
        'rwkv_g2': normal((DEPTH, RWKV_GATE_RANK, RWKV_WIDTH), RWKV_GATE_RANK ** -0.5),
        'rwkv_lnx_w': gain((DEPTH, RWKV_WIDTH)),
        'rwkv_lnx_b': normal((DEPTH, RWKV_WIDTH), 0.01),
        'w_out': normal((DEPTH, MIX_WIDTH, D_MODEL), MIX_WIDTH ** -0.5),
        'w_fc1': normal((DEPTH, D_MODEL, MLP_HIDDEN), D_MODEL ** -0.5),
        'w_fc2': normal((DEPTH, MLP_HIDDEN, D_MODEL), MLP_HIDDEN ** -0.5),
        'final_norm_g': gain((D_MODEL,)),
    }


def reference(x, c, ctx, c_ctx, w_mod, b_mod, norm1_g, norm2_g, w_in, rwkv_shift, na_rpb,
              gqa_q_norm, gqa_k_norm, mla_q_norm, mla_kv_norm, mla_w_uq, mla_w_ukv,
              rwkv_w0, rwkv_w2, rwkv_a0, rwkv_a2, rwkv_k_k, rwkv_k_a, rwkv_r_k, rwkv_g2,
              rwkv_lnx_w, rwkv_lnx_b, w_out, w_fc1, w_fc2, final_norm_g):
    n_lat = x.shape[1]
    rows = n_lat // GRID_W
    rope_head = axial_rope_tables(n_lat, HEAD_DIM)
    rope_mla = axial_rope_tables(n_lat, MLA_ROPE_DIM)
    silu_c = jax.nn.silu(c)
    silu_c_ctx = jax.nn.silu(c_ctx)
    xc = ctx
    for layer in range(DEPTH):
        want_ctx = layer < DEPTH - 1
        mod = silu_c @ w_mod[layer] + b_mod[layer]
        sh1, sc1, g1, sh2, sc2, g2 = jnp.split(mod[:, None, :], 6, axis=-1)
        mod_c = silu_c_ctx @ w_mod[layer] + b_mod[layer]
        csh1, csc1, cg1, csh2, csc2, cg2 = jnp.split(mod_c, 6, axis=-1)
        h_lat = rms_norm(x, norm1_g[layer]) * (1.0 + sc1) + sh1
        h_ctx = rms_norm(xc, norm1_g[layer]) * (1.0 + csc1) + csh1
        rwkv_params = (rwkv_w0[layer], rwkv_w2[layer], rwkv_a0[layer], rwkv_a2[layer],
                       rwkv_k_k[layer], rwkv_k_a[layer], rwkv_r_k[layer], rwkv_g2[layer],
                       rwkv_lnx_w[layer], rwkv_lnx_b[layer])
        mix_lat, mix_ctx = hybrid_token_mixer(
            h_lat, h_ctx, rows, w_in[layer], rwkv_shift[layer], na_rpb[layer],
            gqa_q_norm[layer], gqa_k_norm[layer], mla_q_norm[layer], mla_kv_norm[layer],
            mla_w_uq[layer], mla_w_ukv[layer], rwkv_params, rope_head, rope_mla, want_ctx)
        x = x + g1 * (mix_lat @ w_out[layer])
        h2 = rms_norm(x, norm2_g[layer]) * (1.0 + sc2) + sh2
        x = x + g2 * squared_relu_mlp(h2, w_fc1[layer], w_fc2[layer])
        if want_ctx:
            xc = xc + cg1 * (mix_ctx @ w_out[layer])
            h2c = rms_norm(xc, norm2_g[layer]) * (1.0 + csc2) + csh2
            xc = xc + cg2 * squared_relu_mlp(h2c, w_fc1[layer], w_fc2[layer])
    return rms_norm(x, final_norm_g)
```

```python
import numpy as np
import ml_dtypes
from contextlib import ExitStack
import concourse.bass as bass
import concourse.mybir as mybir
from concourse.bass_utils import run_bass_kernel_spmd

F32 = mybir.dt.float32
BF16 = mybir.dt.bfloat16
AF = mybir.ActivationFunctionType
ALU = mybir.AluOpType
AX = mybir.AxisListType

D = 1024
T_CTX = 256
T_LAT = 4096
T = T_CTX + T_LAT
GRID_W = 64
DEPTH = 4
NA_C0, GQA_C0, MLA_C0, RW_C0 = 0, 768, 1280, 1696
IN_COLS = 2880
RMS_EPS = 1e-6
GN_EPS = 64e-5
NEG = -30000.0
BLOCKS = [(0, 256)] + [(256 + 512 * i, 512) for i in range(8)]
NKT = T // 128


def hc(g):
    return g + 1 if g < T_CTX else g + 3


HT_COLS = T + 4

ENGS = ("pe", "act", "dve", "pool", "sp")
N_DMA_SEMS = 24


class Op:
    __slots__ = ("idx", "eng", "fn", "is_dma", "deps", "marked", "mark_val", "dma_sem", "dma_val")

    def __init__(self, idx, eng, fn, is_dma):
        self.idx = idx
        self.eng = eng
        self.fn = fn
        self.is_dma = is_dma
        self.deps = []
        self.marked = False
        self.mark_val = 0
        self.dma_sem = None
        self.dma_val = 0


class Prog:
    def __init__(self, nc):
        self.nc = nc
        self.ops = []
        self.bstate = {}
        self.es = ExitStack()
        self.cur_es = self.es
        self.ntile = 0
        self.bar_start = 0

    def sb(self, shape, dt=F32, name=None):
        self.ntile += 1
        return self.cur_es.enter_context(self.nc.sbuf_tensor(name or f"sb{self.ntile}", list(shape), dt))

    def ps(self, shape, dt=F32, name=None):
        self.ntile += 1
        return self.es.enter_context(self.nc.psum_tensor(name or f"ps{self.ntile}", list(shape), dt))

    def barrier(self):
        ops = self.ops
        last = {}
        dmas = []
        for o in ops[self.bar_start:]:
            if o.is_dma:
                dmas.append(o.idx)
            elif o.fn is not None:
                last[o.eng] = o.idx
        for e in ENGS:
            o = Op(len(ops), e, None, False)
            o.deps = [i for (en, i) in last.items() if en != e] + list(dmas)
            ops.append(o)
        self.bstate = {}
        self.bar_start = len(ops)

    class _Phase:
        def __init__(self, p):
            self.p = p

        def __enter__(self):
            self.es = ExitStack()
            self.prev = self.p.cur_es
            self.p.cur_es = self.es
            return self

        def __exit__(self, *a):
            self.p.barrier()
            self.p.cur_es = self.prev
            self.es.close()
            return False

    def phase(self):
        return Prog._Phase(self)

    def _st(self, b):
        k = b if isinstance(b, str) else id(b)
        st = self.bstate.get(k)
        if st is None:
            st = [None, {}, []]
            self.bstate[k] = st
        return st

    def op(self, eng, fn, reads=(), writes=(), dma=False):
        o = Op(len(self.ops), eng, fn, dma)
        hard = set()
        war = set()
        for b in reads:
            st = self._st(b)
            if st[0] is not None:
                hard.add(st[0])
        for b in writes:
            st = self._st(b)
            if st[0] is not None:
                hard.add(st[0])
            for r in st[1].values():
                war.add(r)
            for r in st[2]:
                war.add(r)
        for b in reads:
            st = self._st(b)
            if dma:
                st[2].append(o.idx)
            else:
                st[1][eng] = o.idx
        for b in writes:
            st = self._st(b)
            st[0] = o.idx
            st[1] = {}
            st[2] = []
        deps = []
        for d in hard | war:
            if d == o.idx:
                continue
            od = self.ops[d]
            if (not od.is_dma) and (not dma) and od.eng == eng:
                if eng == "pe":
                    continue
                if d not in hard:
                    continue
            deps.append(d)
        o.deps = deps
        self.ops.append(o)
        return o

    def dma(self, out, in_, reads=(), writes=(), q="sp", **kw):
        return self.op(q, lambda e: e.dma_start(out=out, in_=in_, **kw), reads, writes, dma=True)

    def mm(self, out, lhsT, rhs, start=True, stop=True, reads=(), writes=(), **kw):
        return self.op("pe", lambda e: e.matmul(out, lhsT, rhs, start=start, stop=stop, **kw), reads, writes)

    def tr(self, out, in_, ident, reads=(), writes=()):
        return self.op("pe", lambda e: e.transpose(out, in_, ident), reads, writes)

    def act(self, out, in_, func, reads=(), writes=(), **kw):
        return self.op("act", lambda e: e.activation(out, in_, func, **kw), reads, writes)

    def tt(self, out, in0, in1, op, reads=(), writes=(), eng="dve"):
        return self.op(eng, lambda e: e.tensor_tensor(out, in0, in1, op), reads, writes)

    def ts(self, out, in0, s1, s2, op0, op1=None, reads=(), writes=(), eng="dve"):
        if op1 is None:
            return self.op(eng, lambda e: e.tensor_scalar(out, in0, s1, None, op0), reads, writes)
        return self.op(eng, lambda e: e.tensor_scalar(out, in0, s1, s2, op0, op1), reads, writes)

    def stt(self, out, in0, scalar, in1, op0, op1, reads=(), writes=()):
        return self.op("dve", lambda e: e.scalar_tensor_tensor(out, in0, scalar, in1, op0, op1), reads, writes)

    def copy(self, out, in_, reads=(), writes=(), eng="dve"):
        if eng == "act":
            return self.op("act", lambda e: e.copy(out, in_), reads, writes)
        return self.op(eng, lambda e: e.tensor_copy(out, in_), reads, writes)

    def recip(self, out, in_, reads=(), writes=()):
        return self.op("dve", lambda e: e.reciprocal(out, in_), reads, writes)

    def memset(self, ap, val, writes=(), eng="dve"):
        return self.op(eng, lambda e: e.memset(ap, val), (), writes)

    def emit(self):
        nc = self.nc
        ops = self.ops
        for o in ops:
            for d in o.deps:
                ops[d].marked = True
        cnt = {e: 0 for e in ENGS}
        dma_cnt = {e: 0 for e in ENGS}
        dma_uses = {}
        for o in ops:
            if o.is_dma:
                k = dma_cnt[o.eng]
                dma_cnt[o.eng] += 1
                s = (o.eng, k % N_DMA_SEMS)
                dma_uses[s] = dma_uses.get(s, 0) + 1
                o.dma_sem = s
                o.dma_val = 16 * dma_uses[s]
            elif o.marked:
                cnt[o.eng] += 1
                o.mark_val = cnt[o.eng]
        es = self.es
        esem = {e: es.enter_context(nc.semaphore(f"s_{e}")) for e in ENGS if e != "sp"}
        dsem = {}
        for q in ENGS:
            for i in range(min(N_DMA_SEMS, dma_cnt[q])):
                dsem[(q, i)] = es.enter_context(nc.semaphore(f"d_{q}{i}"))
        block = es.enter_context(nc.Block())
        by_eng = {e: [o for o in ops if o.eng == e] for e in ENGS}
        final_dma = dict((s, 16 * n) for s, n in dma_uses.items())

        def run(ename, eobj):
            known = {}

            def wait(key, sem, val):
                if known.get(key, 0) >= val:
                    return
                eobj.wait_ge(sem, val)
                known[key] = val

            for o in by_eng[ename]:
                for d in o.deps:
                    od = ops[d]
                    if od.is_dma:
                        wait(od.dma_sem, dsem[od.dma_sem], od.dma_val)
                    else:
                        wait(od.eng, esem[od.eng], od.mark_val)
                if o.fn is None:
                    continue
                if o.is_dma:
                    if o.dma_val > 16:
                        wait(o.dma_sem, dsem[o.dma_sem], o.dma_val - 16)
                    o.fn(eobj).then_inc(dsem[o.dma_sem], 16)
                else:
                    ins = o.fn(eobj)
                    if o.marked:
                        ins.then_inc(esem[ename], 1)
            for s, v in final_dma.items():
                if s[0] == ename:
                    wait(s, dsem[s], v)

        if by_eng["sp"]:
            @block.sync
            def _(e):
                run("sp", e)
        if by_eng["pe"]:
            @block.tensor
            def _(e):
                run("pe", e)
        if by_eng["act"]:
            @block.scalar
            def _(e):
                run("act", e)
        if by_eng["dve"]:
            @block.vector
            def _(e):
                run("dve", e)
        if by_eng["pool"]:
            @block.gpsimd
            def _(e):
                run("pool", e)

    def close(self):
        self.es.close()


def featT(v):
    v = np.asarray(v)
    n = v.shape[-1] // 128
    return np.ascontiguousarray(np.moveaxis(v.reshape(v.shape[:-1] + (n, 128)), -1, -2))


def rope_tables(rot_dim):
    t = np.arange(T_LAT)
    rows = (t // GRID_W).astype(np.float64)
    cols = (t % GRID_W).astype(np.float64)
    per_axis = rot_dim // 2
    inv = 10000.0 ** (-np.arange(0, per_axis, 2, dtype=np.float64) / per_axis)
    ang = np.concatenate([rows[:, None] * inv, cols[:, None] * inv], axis=-1)
    cos = np.ones((rot_dim, T), np.float32)
    sin = np.zeros((rot_dim, T), np.float32)
    cos[:, T_CTX:] = np.repeat(np.cos(ang).T, 2, axis=0)
    sin[:, T_CTX:] = np.repeat(np.sin(ang).T, 2, axis=0)
    return cos, sin


def rot_lhsT(n, off=0, size=None):
    size = size or n
    m = np.zeros((size, size), np.float32)
    for i in range(n // 2):
        m[off + 2 * i + 1, off + 2 * i] = -1.0
        m[off + 2 * i, off + 2 * i + 1] = 1.0
    return m


def na_plan():
    rows = T_LAT // GRID_W
    wr, wc = 8, 16
    r_idx = np.arange(rows)
    key_lo = np.clip(r_idx - wr // 2, 0, rows - wr)
    c_idx = np.arange(GRID_W)
    col_start = np.clip(c_idx - wc // 2, 0, GRID_W - wc)
    col_ok = (c_idx[None, :] >= col_start[:, None]) & (c_idx[None, :] < col_start[:, None] + wc)
    masks, mkey, plan = [], {}, []
    for qb in range(8):
        r0 = qb * 8
        lo = max(r0 - 4, 0)
        hi = min(r0 + 12, rows)
        ent = []
        for kr0 in range(lo, hi, 2):
            m = np.full((2, 64, 8, 64), NEG, np.float32)
            for kl in range(2):
                for rl in range(8):
                    r = r0 + rl
                    kr = kr0 + kl
                    if key_lo[r] <= kr < key_lo[r] + wr:
                        m[kl, :, rl, :] = np.where(col_ok.T, 0.0, NEG)
            if not (m == 0).any():
                continue
            key = m.tobytes()
            if key not in mkey:
                mkey[key] = len(masks)
                masks.append(m.reshape(128, 512))
            ent.append((2 + kr0 // 2, kr0 - r0, mkey[key]))
        plan.append(ent)
    return plan, np.stack(masks, 0)


NA_PLAN, NA_MASKS = na_plan()
NM = NA_MASKS.shape[0]


def na_bias_table(rpb):
    L = rpb.shape[0]
    ck = np.arange(64)[:, None]
    cq = np.arange(64)[None, :]
    dc = np.clip(ck - cq + 15, 0, 30)
    out = np.zeros((L, 4, 2, 64, 22, 64), np.float32)
    for kl in range(2):
        for j in range(22):
            dr = 17 - j + kl
            if 0 <= dr <= 14:
                out[:, :, kl, :, j, :] = rpb[:, :, dr][:, :, dc]
    return out.reshape(L, 4, 128, 22, 64)


RW_LOCK = 8


def build(nl=DEPTH, dbg=(), mixers=("na", "gqa", "mla", "rw"), do_mlp=True, force_ctx=None):
    nc = bass.Bass("TRN2", target_bir_lowering=False)
    P = Prog(nc)

    def din(name, shape, dt=F32):
        return nc.dram_tensor(name, list(shape), dt, kind="ExternalInput").ap()

    def dscr(name, shape, dt=F32):
        return nc.dram_tensor(name, list(shape), dt, kind="Internal").ap()

    def dout(name, shape, dt=F32):
        return nc.dram_tensor(name, list(shape), dt, kind="ExternalOutput").ap()

    x_in = din("x", [T_LAT, D])
    ctx_in = din("ctx", [T_CTX, D])
    cvec = din("cvec", [128, 8, 2])
    w_mod = din("w_mod", [DEPTH, D, 6 * D])
    b_modT = din("b_modT", [DEPTH, 128, 48])
    n1g = din("n1g", [DEPTH, 128, 8])
    n2g = din("n2g", [DEPTH, 128, 8])
    fng = din("fng", [128, 8])
    w_in = din("w_in", [DEPTH, D, IN_COLS])
    w_kr96 = din("w_kr96", [DEPTH, D, 96])
    gqa_qn = din("gqa_qn", [DEPTH, 128, 1])
    gqa_kn = din("gqa_kn", [DEPTH, 128, 1])
    mla_qn = din("mla_qn", [DEPTH, 128, 2])
    mla_kvn = din("mla_kvn", [DEPTH, 128, 1])
    w_uq = din("w_uq", [DEPTH, 256, 384])
    w_ukv = din("w_ukv", [DEPTH, 128, 512])
    na_tb = din("na_tb", [DEPTH, 4, 128, 22, 64])
    na_rpbf = din("na_rpbf", [DEPTH, 4, 1, 465])
    na_masks = din("na_masks", [NM, 128, 512], BF16)
    w_out = din("w_out", [DEPTH, D, D])
    w_fc1 = din("w_fc1", [DEPTH, D, 4 * D])
    w_fc2 = din("w_fc2", [DEPTH, 4 * D, D])
    c_ident = din("c_ident", [128, 128])
    c_bd64 = din("c_bd64", [128, 128])
    c_rt128 = din("c_rt128", [128, 128])
    c_rt96 = din("c_rt96", [96, 96])
    c_cos128 = din("c_cos128", [128, T])
    c_sin128 = din("c_sin128", [128, T])
    c_cos96 = din("c_cos96", [96, T])
    c_sin96 = din("c_sin96", [96, T])
    rw_tap = din("rw_tap", [DEPTH, 128, 10, 3])
    rw_vec = din("rw_vec", [DEPTH, 128, 5, 2, 2])
    rw_w2 = din("rw_w2", [DEPTH, 128, 256])
    rw_a2 = din("rw_a2", [DEPTH, 128, 256])
    rw_g2 = din("rw_g2", [DEPTH, 160, 256])
    rw_ln = din("rw_ln", [DEPTH, 128, 2, 2])
    c_tri = din("c_tri", [4, 128, 128])
    c_blk = din("c_blk", [4, 128, 128])
    y_out = dout("y", [T_LAT, D])

    xT = dscr("xT", [8, 128, T])
    h2T = dscr("h2T", [8, 128, T], BF16)
    mixT = dscr("mixT", [8, 128, T], BF16)
    na_QT = dscr("na_QT", [4, 64, T], BF16); na_KT = dscr("na_KT", [4, 64, T], BF16); na_V = dscr("na_V", [T, 256], BF16)
    gq_QT = dscr("gq_QT", [4, 64, T], BF16); gq_KT = dscr("gq_KT", [2, 64, T], BF16); gq_V = dscr("gq_V", [T, 128], BF16)
    ml_QT = dscr("ml_QT", [4, 96, T], BF16); ml_KT = dscr("ml_KT", [4, 96, T], BF16); ml_V = dscr("ml_V", [T, 256], BF16)

    rw_fm = dscr("rw_fm", [2, 4, 2, 128, T])
    rw_tm = dscr("rw_tm", [2, 2, T, 256])
    rw_v = dscr("rw_v", [T, 256])
    rw_vT = dscr("rw_vT", [2, 128, T])
    rw_gate = dscr("rw_gate", [2, 128, T])
    rw_zz = dscr("rw_zz", [2, 128, T])

    pcs = P.sb([128, 2, 2, NKT])
    ident = P.sb([128, 128]); ones_f = P.sb([128, 128]); bd64 = P.sb([128, 128])
    ones_bf = P.sb([128, 128], BF16)
    rt128 = P.sb([128, 128]); rt96 = P.sb([96, 96])
    cs = P.sb([128, 8, 2])
    modT = P.sb([128, 48, 2])
    gs = P.sb([128, 8, 2]); sh = P.sb([128, 8, 2])
    bm = P.sb([128, 48]); ng1 = P.sb([128, 8]); ng2 = P.sb([128, 8]); fg = P.sb([128, 8])
    eps_t = P.sb([128, 1])
    stat = P.sb([128, 3, 2, 4, 9])
    negm = P.sb([128, 12])
    smallv = P.sb([128, 8])
    pb = [P.ps([128, 512]) for _ in range(8)]
    cnt = {}

    def nxt(name, arr):
        i = cnt.get(name, 0)
        cnt[name] = i + 1
        return arr[i % len(arr)]

    def npb():
        return nxt("pb", pb)

    P.dma(ident[:], c_ident, writes=[ident])
    P.dma(bd64[:], c_bd64, writes=[bd64])
    P.dma(rt128[:], c_rt128, writes=[rt128])
    P.dma(rt96[:], c_rt96, writes=[rt96])
    P.dma(fg[:], fng, writes=[fg])
    P.memset(ones_f[:], 1.0, writes=[ones_f])
    P.memset(ones_bf[:], 1.0, writes=[ones_bf])
    P.memset(eps_t[:], RMS_EPS, writes=[eps_t])
    P.dma(cs[:], cvec, writes=[cs])
    P.act(cs[:], cs[:], AF.Silu, reads=[cs], writes=[cs])

    def dump(name, ap, shape, reads, dt=F32):
        if name in dbg or name.split("_")[0][:2] == "rw" and any(x.startswith("rw") for x in dbg):
            d = dout("dbg_" + name, shape, dt)
            P.dma(d, ap, reads=reads, q="pool")

    def norm_block(xb, n, out_fn, scale_fn, bias_fn, T_):
        sq = nxt("big", T_["big"])
        P.act(sq[:, :, 0:n], xb[:, :, 0:n], AF.Square, reads=[xb], writes=[sq])
        ps = npb()
        for ft in range(8):
            P.mm(ps[:, 0:n], ones_f[:], sq[:, ft, 0:n], start=(ft == 0), stop=(ft == 7), reads=[ones_f, sq], writes=[ps])
        rs = nxt("rs", T_["rs"])
        P.act(rs[:, 0:n], ps[:, 0:n], AF.Sqrt, reads=[ps, eps_t], writes=[rs], scale=1.0 / D, bias=eps_t[:, 0:1])
        P.recip(rs[:, 0:n], rs[:, 0:n], reads=[rs], writes=[rs])
        for ft in range(8):
            tmp = nxt("tmp", T_["tmp"])
            P.tt(tmp[:, 0:n], xb[:, ft, 0:n], rs[:, 0:n], ALU.mult, reads=[xb, rs], writes=[tmp])
            dst, wk = out_fn(ft)
            b = bias_fn(ft)
            if b is None:
                P.act(dst, tmp[:, 0:n], AF.Identity, reads=[tmp, gs, sh, fg], writes=wk, scale=scale_fn(ft))
            else:
                P.act(dst, tmp[:, 0:n], AF.Identity, reads=[tmp, gs, sh, fg], writes=wk, scale=scale_fn(ft), bias=b)

    def set_norm_params(which):
        base = 0 if which == 1 else 24
        ng = ng1 if which == 1 else ng2
        for s in range(2):
            P.ts(gs[:, :, s], modT[:, base + 8:base + 16, s], 1.0, None, ALU.add, reads=[modT], writes=[gs])
            P.tt(gs[:, :, s], gs[:, :, s], ng[:], ALU.mult, reads=[gs, ng], writes=[gs])
            P.copy(sh[:, :, s], modT[:, base:base + 8, s], reads=[modT], writes=[sh])

    def load_w_bf16(dst, dkey, src, K, ncols, T_, engs=("dve", "pool")):
        kt_n = K // 128
        per = 4096 // kt_n
        c = 0
        i = 0
        while c < ncols:
            w = min(per, ncols - c)
            st = nxt("big", T_["big"])
            sv = st[:].rearrange("p a t -> p (a t)")[:, 0:kt_n * w].rearrange("p (k n) -> p k n", k=kt_n)
            P.dma(sv, src[:, c:c + w].rearrange("(kt p) n -> p kt n", p=128), writes=[st])
            P.copy(dst[:, :, c:c + w], sv, reads=[st], writes=[dkey], eng=engs[i % len(engs)])
            c += w
            i += 1

    def stage_A():
        with P.phase():
            big = [P.sb([128, 8, 512]) for _ in range(4)]
            for bi, (g0, n) in enumerate(BLOCKS):
                na = n // 128
                xin = nxt("bigA", big)
                src = ctx_in if g0 < T_CTX else x_in[g0 - T_CTX:g0 - T_CTX + n, :]
                xv = xin[:].rearrange("p a t -> p (a t)")[:, 0:na * 1024].rearrange("p (a f) -> p a f", a=na)
                P.dma(xv, src.rearrange("(a p) f -> p a f", p=128), writes=[xin])
                xb = nxt("bigA", big)
                for ft in range(8):
                    ps = npb()
                    for a in range(na):
                        P.tr(ps[:, a * 128:(a + 1) * 128], xv[:, a, ft * 128:(ft + 1) * 128], ident[:],
                             reads=[xin, ident], writes=[ps])
                    P.copy(xb[:, ft, 0:n], ps[:, 0:n], reads=[ps], writes=[xb], eng=("act" if ft % 2 else "dve"))
                P.dma(xT[:, :, g0:g0 + n].rearrange("a p t -> p a t"), xb[:, :, 0:n], reads=[xb], writes=[f"xT{bi}"], q="pool")

    def stage_M(L, T_):
        P.dma(bm[:], b_modT[L], writes=[bm])
        P.dma(ng1[:], n1g[L], writes=[ng1])
        P.dma(ng2[:], n2g[L], writes=[ng2])
        for jc in range(12):
            wm = nxt("big", T_["big"])
            P.dma(wm[:], w_mod[L][:, jc * 512:(jc + 1) * 512].rearrange("(kt p) n -> p kt n", p=128), writes=[wm])
            ps = npb()
            for j4 in range(4):
                for kt in range(8):
                    P.mm(ps[:, j4 * 2:j4 * 2 + 2], wm[:, kt, j4 * 128:(j4 + 1) * 128], cs[:, kt, :],
                         start=(kt == 0), stop=(kt == 7), reads=[wm, cs], writes=[ps])
            for j4 in range(4):
                j = jc * 4 + j4
                P.ts(modT[:, j, :], ps[:, j4 * 2:j4 * 2 + 2], bm[:, j:j + 1], None, ALU.add,
                     reads=[ps, bm], writes=[modT])

    def phase1(L):
        with P.phase():
            hT = P.sb([128, 8, HT_COLS], BF16)
            P.memset(hT[:].rearrange("p a t -> p (a t)"), 0.0, writes=[hT], eng="pool")
            phase1a(L, hT)
            if "rw" in mixers:
                rw_prep(L, hT)

    def phase1a(L, hT):
        with P.phase():
            T_ = {"big": [P.sb([128, 8, 512]) for _ in range(2)],
                  "tmp": [P.sb([128, 512]) for _ in range(6)],
                  "rs": [P.sb([128, 512]) for _ in range(2)]}
            wb = P.sb([128, 8, 768], BF16)
            ob = [P.sb([128, 512], BF16) for _ in range(4)]
            cst = [P.sb([128, 512]) for _ in range(4)]
            stage_M(L, T_)
            set_norm_params(1)
            P.dma(smallv[:, 0:1], gqa_qn[L], writes=[smallv])
            P.dma(smallv[:, 1:2], gqa_kn[L], writes=[smallv])
            P.dma(smallv[:, 2:4], mla_qn[L], writes=[smallv])
            P.dma(smallv[:, 4:5], mla_kvn[L], writes=[smallv])
            for bi, (g0, n) in enumerate(BLOCKS):
                xb = nxt("big", T_["big"])
                P.dma(xb[:, :, 0:n], xT[:, :, g0:g0 + n].rearrange("a p t -> p a t"), reads=[f"xT{bi}"], writes=[xb])
                s = 1 if g0 < T_CTX else 0
                c0 = hc(g0)
                norm_block(xb, n, (lambda ft, c0=c0, n=n: (hT[:, ft, c0:c0 + n], [hT])),
                           (lambda ft, s=s: gs[:, ft, s:s + 1]), (lambda ft, s=s: sh[:, ft, s:s + 1]), T_)
            dump("hT", hT[:], [128, 8, HT_COLS], [hT], BF16)
            dump("modT", modT[:], [128, 48, 2], [modT])

            def proj_fm(c0, m, g0, n, ps, wt=wb, wkey="wb"):
                c = hc(g0)
                for kt in range(8):
                    P.mm(ps[0:m, 0:n], wt[:, kt, c0:c0 + m], hT[:, kt, c:c + n], start=(kt == 0), stop=(kt == 7),
                         reads=[wkey, hT], writes=[ps])

            def proj_tm(c0, ncols, g, ps, col_off=0):
                c = hc(g)
                for kt in range(8):
                    P.mm(ps[:, col_off:col_off + ncols], hT[:, kt, c:c + 128], wb[:, kt, c0:c0 + ncols],
                         start=(kt == 0), stop=(kt == 7), reads=["wb", hT], writes=[ps])

            def norm2_stat(src_ap, rows, n, mixer, qk, head, bi, skey, base=0):
                sq = nxt("tmp", T_["tmp"])
                P.act(sq[base:base + rows, 0:n], src_ap, AF.Square, reads=[skey], writes=[sq])
                ps = npb()
                P.mm(ps[:, 0:n], ones_f[base:base + rows, :], sq[base:base + rows, 0:n], reads=[ones_f, sq], writes=[ps])
                P.op("dve", lambda e: e.reduce_max(stat[:, mixer, qk, head, bi:bi + 1], ps[:, 0:n], AX.X),
                     reads=[ps], writes=[stat])

            def store_v(ps, ncols, g, dst):
                o = nxt("ob", ob)
                P.copy(o[:, 0:ncols], ps[:, 0:ncols], reads=[ps], writes=[o], eng="act")
                P.dma(dst[g:g + 128, :], o[:, 0:ncols], reads=[o], writes=[], q="pool")

            if "na" in mixers:
                load_w_bf16(wb[:, :, 0:768], "wb", w_in[L][:, NA_C0:NA_C0 + 768], D, 768, T_)
                for bi, (g0, n) in enumerate(BLOCKS):
                    for qk, dst in ((0, na_QT), (1, na_KT)):
                        for j in range(2):
                            ps = npb()
                            proj_fm(qk * 256 + j * 128, 128, g0, n, ps)
                            o = nxt("ob", ob)
                            P.copy(o[:, 0:n], ps[:, 0:n], reads=[ps], writes=[o], eng="act")
                            P.dma(dst[2 * j:2 * j + 2, :, g0:g0 + n].rearrange("h d t -> (h d) t"), o[:, 0:n], reads=[o], q="pool")
                            for hh in range(2):
                                norm2_stat(o[hh * 64:(hh + 1) * 64, 0:n], 64, n, 0, qk, 2 * j + hh, bi, o, base=hh * 64)
                    for a in range(n // 128):
                        ps = npb()
                        proj_tm(512, 256, g0 + a * 128, ps)
                        store_v(ps, 256, g0 + a * 128, na_V)

            if "gqa" in mixers:
                load_w_bf16(wb[:, :, 0:512], "wb", w_in[L][:, GQA_C0:GQA_C0 + 512], D, 512, T_)
                for bi, (g0, n) in enumerate(BLOCKS):
                    ct_ = nxt("cst", cst); st_ = nxt("cst", cst)
                    P.dma(ct_[:, 0:n], c_cos128[:, g0:g0 + n], writes=[ct_])
                    P.dma(st_[:, 0:n], c_sin128[:, g0:g0 + n], writes=[st_])
                    for (c0, dst, h0, gcol, qk) in ((0, gq_QT, 0, 0, 0), (128, gq_QT, 2, 0, 0), (256, gq_KT, 0, 1, 1)):
                        ps = npb()
                        proj_fm(c0, 128, g0, n, ps)
                        sq = nxt("tmp", T_["tmp"])
                        P.act(sq[:, 0:n], ps[:, 0:n], AF.Square, reads=[ps], writes=[sq])
                        ps2 = npb()
                        P.mm(ps2[:, 0:n], bd64[:], sq[:, 0:n], reads=[bd64, sq], writes=[ps2])
                        rs = nxt("rs", T_["rs"])
                        P.act(rs[:, 0:n], ps2[:, 0:n], AF.Sqrt, reads=[ps2, eps_t], writes=[rs], scale=1.0 / 64, bias=eps_t[:, 0:1])
                        P.recip(rs[:, 0:n], rs[:, 0:n], reads=[rs], writes=[rs])
                        qn = nxt("tmp", T_["tmp"])
                        P.stt(qn[:, 0:n], ps[:, 0:n], smallv[:, gcol:gcol + 1], rs[:, 0:n], ALU.mult, ALU.mult,
                              reads=[ps, smallv, rs], writes=[qn])
                        ps3 = npb()
                        P.mm(ps3[:, 0:n], rt128[:], qn[:, 0:n], reads=[rt128, qn], writes=[ps3])
                        t1 = nxt("tmp", T_["tmp"])
                        P.tt(t1[:, 0:n], qn[:, 0:n], ct_[:, 0:n], ALU.mult, reads=[qn, ct_], writes=[t1])
                        t2 = nxt("tmp", T_["tmp"])
                        P.tt(t2[:, 0:n], ps3[:, 0:n], st_[:, 0:n], ALU.mult, reads=[ps3, st_], writes=[t2])
                        o = nxt("ob", ob)
                        P.tt(o[:, 0:n], t1[:, 0:n], t2[:, 0:n], ALU.add, reads=[t1, t2], writes=[o])
                        P.dma(dst[h0:h0 + 2, :, g0:g0 + n].rearrange("h d t -> (h d) t"), o[:, 0:n], reads=[o], q="pool")
                        for hh in range(2):
                            norm2_stat(o[hh * 64:(hh + 1) * 64, 0:n], 64, n, 1, qk, h0 + hh, bi, o, base=hh * 64)
                    for a in range(n // 128):
                        ps = npb()
                        proj_tm(384, 128, g0 + a * 128, ps)
                        store_v(ps, 128, g0 + a * 128, gq_V)

            if "mla" in mixers:
                load_w_bf16(wb[:, :, 0:384], "wb", w_in[L][:, MLA_C0:MLA_C0 + 384], D, 384, T_)
                load_w_bf16(wb[:, :, 384:480], "wb", w_kr96[L], D, 96, T_)
                wuq = P.sb([128, 2, 384], BF16)
                wukv = P.sb([128, 1, 512], BF16)
                load_w_bf16(wuq[:], wuq, w_uq[L], 256, 384, T_)
                load_w_bf16(wukv[:], wukv, w_ukv[L], 128, 512, T_)
                qlnb = [P.sb([128, 2, 512], BF16) for _ in range(2)]
                ckb = [P.sb([128, 512], BF16) for _ in range(2)]
                for bi, (g0, n) in enumerate(BLOCKS):
                    ct_ = nxt("cst", cst); st_ = nxt("cst", cst)
                    P.dma(ct_[0:96, 0:n], c_cos96[:, g0:g0 + n], writes=[ct_])
                    P.dma(st_[0:96, 0:n], c_sin96[:, g0:g0 + n], writes=[st_])

                    def rope96(src_ps, skey):
                        qf = nxt("tmp", T_["tmp"])
                        P.copy(qf[0:96, 0:n], src_ps[0:96, 0:n], reads=[skey], writes=[qf], eng="act")
                        ps3 = npb()
                        P.mm(ps3[0:96, 0:n], rt96[:], qf[0:96, 0:n], reads=[rt96, qf], writes=[ps3])
                        t1 = nxt("tmp", T_["tmp"])
                        P.tt(t1[0:96, 0:n], qf[0:96, 0:n], ct_[0:96, 0:n], ALU.mult, reads=[qf, ct_], writes=[t1])
                        t2 = nxt("tmp", T_["tmp"])
                        P.tt(t2[0:96, 0:n], ps3[0:96, 0:n], st_[0:96, 0:n], ALU.mult, reads=[ps3, st_], writes=[t2])
                        P.tt(t1[0:96, 0:n], t1[0:96, 0:n], t2[0:96, 0:n], ALU.add, reads=[t1, t2], writes=[t1])
                        return t1

                    psq = [npb(), npb()]
                    sqs = []
                    for ft in range(2):
                        proj_fm(ft * 128, 128, g0, n, psq[ft])
                        sq = nxt("tmp", T_["tmp"])
                        P.act(sq[:, 0:n], psq[ft][:, 0:n], AF.Square, reads=[psq[ft]], writes=[sq])
                        sqs.append(sq)
                    ps2 = npb()
                    for ft in range(2):
                        P.mm(ps2[:, 0:n], ones_f[:], sqs[ft][:, 0:n], start=(ft == 0), stop=(ft == 1), reads=[ones_f, sqs[ft]], writes=[ps2])
                    rs = nxt("rs", T_["rs"])
                    P.act(rs[:, 0:n], ps2[:, 0:n], AF.Sqrt, reads=[ps2, eps_t], writes=[rs], scale=1.0 / 256, bias=eps_t[:, 0:1])
                    P.recip(rs[:, 0:n], rs[:, 0:n], reads=[rs], writes=[rs])
                    ql = nxt("qlnb", qlnb)
                    for ft in range(2):
                        P.stt(ql[:, ft, 0:n], psq[ft][:, 0:n], smallv[:, 2 + ft:3 + ft], rs[:, 0:n], ALU.mult, ALU.mult,
                              reads=[psq[ft], smallv, rs], writes=[ql])
                    for h in range(4):
                        ps = npb()
                        for ft in range(2):
                            P.mm(ps[0:96, 0:n], wuq[:, ft, h * 96:(h + 1) * 96], ql[:, ft, 0:n], start=(ft == 0), stop=(ft == 1),
                                 reads=[wuq, ql], writes=[ps])
                        y = rope96(ps, ps)
                        o = nxt("ob", ob)
                        P.copy(o[0:96, 0:n], y[0:96, 0:n], reads=[y], writes=[o], eng="pool")
                        P.dma(ml_QT[h, :, g0:g0 + n], o[0:96, 0:n], reads=[o], q="pool")
                        norm2_stat(y[0:96, 0:n], 96, n, 2, 0, h, bi, y)
                    psc = npb()
                    proj_fm(256, 128, g0, n, psc)
                    sq = nxt("tmp", T_["tmp"])
                    P.act(sq[:, 0:n], psc[:, 0:n], AF.Square, reads=[psc], writes=[sq])
                    ps2 = npb()
                    P.mm(ps2[:, 0:n], ones_f[:], sq[:, 0:n], reads=[ones_f, sq], writes=[ps2])
                    rs = nxt("rs", T_["rs"])
                    P.act(rs[:, 0:n], ps2[:, 0:n], AF.Sqrt, reads=[ps2, eps_t], writes=[rs], scale=1.0 / 128, bias=eps_t[:, 0:1])
                    P.recip(rs[:, 0:n], rs[:, 0:n], reads=[rs], writes=[rs])
                    ck = nxt("ckb", ckb)
                    P.stt(ck[:, 0:n], psc[:, 0:n], smallv[:, 4:5], rs[:, 0:n], ALU.mult, ALU.mult, reads=[psc, smallv, rs], writes=[ck])
                    psk = npb()
                    proj_fm(384, 96, g0, n, psk)
                    kr = rope96(psk, psk)
                    for h in range(4):
                        ps = npb()
                        P.mm(ps[0:64, 0:n], wukv[:, 0, h * 64:(h + 1) * 64], ck[:, 0:n], reads=[wukv, ck], writes=[ps])
                        kf = nxt("tmp", T_["tmp"])
                        P.copy(kf[0:96, 0:n], kr[0:96, 0:n], reads=[kr], writes=[kf], eng="pool")
                        P.copy(kf[0:64, 0:n], ps[0:64, 0:n], reads=[ps], writes=[kf], eng="act")
                        o = nxt("ob", ob)
                        P.copy(o[0:96, 0:n], kf[0:96, 0:n], reads=[kf], writes=[o], eng="pool")
                        P.dma(ml_KT[h, :, g0:g0 + n], o[0:96, 0:n], reads=[o], q="pool")
                        norm2_stat(kf[0:96, 0:n], 96, n, 2, 1, h, bi, kf)
                    for a in range(n // 128):
                        ps = npb()
                        P.mm(ps[:, 0:256], ck[:, a * 128:(a + 1) * 128], wukv[:, 0, 256:512], reads=[ck, wukv], writes=[ps])
                        store_v(ps, 256, g0 + a * 128, ml_V)

            red = P.sb([128, 3, 2, 4])
            P.op("dve", lambda e: e.tensor_reduce(red[:].rearrange("p a b c -> p (a b c)"),
                                                  stat[:].rearrange("p a b c d -> p (a b c) d"), AX.X, ALU.max),
                 reads=[stat], writes=[red])
            for mi, (mx, scale) in enumerate((("na", 64 ** -0.5), ("gqa", 64 ** -0.5), ("mla", 96 ** -0.5))):
                if mx not in mixers:
                    continue
                for h in range(4):
                    kh = h // 2 if mx == "gqa" else h
                    col = mi * 4 + h
                    P.tt(negm[:, col:col + 1], red[:, mi, 0, h:h + 1], red[:, mi, 1, kh:kh + 1], ALU.mult, reads=[red], writes=[negm])
                    P.act(negm[:, col:col + 1], negm[:, col:col + 1], AF.Sqrt, reads=[negm], writes=[negm], scale=scale * scale)
                    if mx == "na":
                        rp = nxt("tmp", T_["tmp"])
                        P.dma(rp[:, 0:465], na_rpbf[L, h].partition_broadcast(128), writes=[rp])
                        bmx = nxt("rs", T_["rs"])
                        P.op("dve", lambda e, bmx=bmx, rp=rp: e.tensor_reduce(bmx[:, 0:1], rp[:, 0:465], AX.X, ALU.max, apply_absolute_value=True),
                             reads=[rp], writes=[bmx])
                        P.tt(negm[:, col:col + 1], negm[:, col:col + 1], bmx[:, 0:1], ALU.add, reads=[negm, bmx], writes=[negm])
                    P.ts(negm[:, col:col + 1], negm[:, col:col + 1], -1.0, None, ALU.mult, reads=[negm], writes=[negm])
            dump("negm", negm[:], [128, 12], [negm])

    def phase2(L, want_ctx):
        with P.phase():
            QTt = [P.sb([128, T], BF16) for _ in range(2)]
            KTt = [P.sb([128, T], BF16) for _ in range(2)]
            for t__ in QTt + KTt:
                P.memset(t__[:], 0.0, writes=[t__], eng="pool")
            Vt = P.sb([128, NKT, 256], BF16)
            Va = P.sb([128, NKT, 4, 128], BF16)
            pT = [P.sb([128, 512], BF16) for _ in range(3)]
            rl = [P.sb([64, 512]) for _ in range(2)]
            ob = [P.sb([64, 512], BF16) for _ in range(2)]
            tb = [P.sb([128, 22, 64]) for _ in range(2)]
            mk = [P.sb([128, 512], BF16) for _ in range(3)]
            bt = [P.sb([128, 512]) for _ in range(3)]
            sps = pb[0:4]
            ops_ = pb[4:7]
            P.memset(Va[:].rearrange("p a b c -> p (a b c)"), 1.0, writes=[Va], eng="pool")
            specs = []
            if "na" in mixers:
                specs.append(("na", 0, na_QT, na_KT, na_V, 4, 64, 64 ** -0.5))
            if "gqa" in mixers:
                specs.append(("gqa", 1, gq_QT, gq_KT, gq_V, 2, 64, 64 ** -0.5))
            if "mla" in mixers:
                specs.append(("mla", 2, ml_QT, ml_KT, ml_V, 4, 96, 96 ** -0.5))
            for (mx, mi, dQ, dK, dV, nkv, dk, scale) in specs:
                P.dma(Vt[:, :, 0:nkv * 64], dV.rearrange("(kt p) c -> p kt c", p=128), writes=[Vt])
                for kh_ in range(nkv):
                    P.copy(Va[:, :, kh_, 0:64], Vt[:, :, kh_ * 64:(kh_ + 1) * 64], reads=[Vt], writes=[Va], eng=("pool" if kh_ % 2 else "dve"))
                for h in range(4):
                    QT = nxt("QT", QTt); KT = nxt("KT", KTt)
                    kh = h // 2 if mx == "gqa" else h
                    P.dma(QT[0:dk, :], dQ[h], writes=[QT])
                    P.dma(KT[0:dk, :], dK[kh], writes=[KT])
                    col = mi * 4 + h
                    if mx == "na":
                        tbh = nxt("tb", tb)
                        P.dma(tbh[:], na_tb[L, h], writes=[tbh])
                    for bi, (g0, n) in enumerate(BLOCKS):
                        if g0 < T_CTX:
                            if not want_ctx:
                                continue
                            kts = [(0, None), (1, None)]
                        elif mx == "na":
                            kts = [(0, None), (1, None)] + [(kt, (Dd, mi_)) for (kt, Dd, mi_) in NA_PLAN[bi - 1]]
                        else:
                            kts = [(kt, None) for kt in range(NKT)]
                        o_ps = nxt("ops", ops_)
                        nk = len(kts)

                        def issue_S(i):
                            sp = nxt("sps", sps)
                            kt = kts[i][0]
                            P.mm(sp[:, 0:n], KT[:, kt * 128:(kt + 1) * 128], QT[:, g0:g0 + n], reads=[KT, QT], writes=[sp])
                            return sp
                        S = [issue_S(i) for i in range(min(3, nk))]
                        for i, (kt, bias) in enumerate(kts):
                            sp = S[i]
                            p = nxt("pT", pT)
                            if bias is None:
                                P.act(p[:, 0:n], sp[:, 0:n], AF.Exp, reads=[sp, negm], writes=[p], scale=scale, bias=negm[:, col:col + 1])
                            else:
                                Dd, mi_ = bias
                                m_ = nxt("mk", mk)
                                P.dma(m_[:], na_masks[mi_], writes=[m_])
                                b_ = nxt("bt", bt)
                                j0 = 10 - Dd
                                P.tt(b_[:], tbh[:, j0:j0 + 8, :].rearrange("p a b -> p (a b)"), m_[:], ALU.add, reads=[tbh, m_], writes=[b_], eng="pool")
                                P.stt(b_[:], sp[:, 0:n], scale, b_[:], ALU.mult, ALU.add, reads=[sp, b_], writes=[b_])
                                P.act(p[:, 0:n], b_[:, 0:n], AF.Exp, reads=[b_, negm], writes=[p], scale=1.0, bias=negm[:, col:col + 1])
                            if i + 3 < nk:
                                S.append(issue_S(i + 3))
                            P.mm(o_ps[:, 0:n], Va[:, kt, kh, :], p[:, 0:n], start=(i == 0), stop=(i == nk - 1),
                                 reads=[Va, p], writes=[o_ps])
                        r_ = nxt("rl", rl)
                        P.recip(r_[:, 0:n], o_ps[64:128, 0:n], reads=[o_ps], writes=[r_])
                        o = nxt("ob2", ob)
                        P.tt(o[:, 0:n], o_ps[0:64, 0:n], r_[:, 0:n], ALU.mult, reads=[o_ps, r_], writes=[o])
                        f = mi * 256 + h * 64
                        P.dma(mixT[f // 128, f % 128:f % 128 + 64, g0:g0 + n], o[:, 0:n], reads=[o], writes=[f"mix{f}_{bi}"], q="pool")

    RWB = [(0, 256)] + [(256 + 256 * i, 256) for i in range(16)]
    DECAY_C = -0.6065306597126334

    def rw_prep(L, hT):
        with P.phase():
            wr = P.sb([128, 8, 1184], BF16)
            with P.phase():
                big = {"big": [P.sb([128, 8, 512]) for _ in range(2)]}
                load_w_bf16(wr[:], wr, w_in[L][:, RW_C0:RW_C0 + 1184], D, 1184, big)
            tap = P.sb([128, 10, 3]); vec = P.sb([128, 5, 2, 2]); w2s = P.sb([128, 256]); a2s = P.sb([128, 256])
            g2a = P.sb([128, 256]); g2b = P.sb([32, 256])
            P.dma(tap[:], rw_tap[L], writes=[tap]); P.dma(vec[:], rw_vec[L], writes=[vec])
            P.dma(w2s[:], rw_w2[L], writes=[w2s]); P.dma(a2s[:], rw_a2[L], writes=[a2s])
            P.dma(g2a[:], rw_g2[L][0:128, :], writes=[g2a]); P.dma(g2b[:], rw_g2[L][128:160, :], writes=[g2b])
            U = [P.sb([128, 10, 256]) for _ in range(2)]
            Rs = [{k: P.sb([128, 256]) for k in ("th", "sg0", "sg1", "lw", "asg", "kkr", "sq", "nr", "kk", "b", "t", "kd",
                                                "cs", "cl", "clx", "e1", "e2", "e3", "bh", "kh", "tz")} for _ in range(2)]
            R = Rs[0]
            gate_t = [P.sb([128, 256]) for _ in range(2)]
            zz = P.sb([128, 2, 256])
            vt = P.sb([128, 2, 256])
            fmout = [P.sb([128, 4, 2, 256]) for _ in range(2)]
            tmout = [P.sb([128, 2, 2, 256]) for _ in range(2)]
            for bi, (g0, n) in enumerate(RWB):
                c = hc(g0)
                u = nxt("U", U)
                uk = lambda ti: f"u{id(u)}_{ti}"
                for ti in range(10):
                    m = 128 if ti < 9 else 32
                    ps = npb()
                    for kt in range(8):
                        P.mm(ps[0:m, 0:n + 2], wr[:, kt, ti * 128:ti * 128 + m], hT[:, kt, c - 1:c + n + 1],
                             start=(kt == 0), stop=(kt == 7), reads=[wr, hT], writes=[ps])
                    P.ts(u[0:m, ti, :], ps[0:m, 0:n], tap[0:m, ti, 0:1], None, ALU.mult, reads=[ps, tap], writes=[uk(ti)])
                    P.stt(u[0:m, ti, :], ps[0:m, 1:n + 1], tap[0:m, ti, 1:2], u[0:m, ti, :], ALU.mult, ALU.add,
                          reads=[ps, tap, uk(ti)], writes=[uk(ti)])
                    P.stt(u[0:m, ti, :], ps[0:m, 2:n + 2], tap[0:m, ti, 2:3], u[0:m, ti, :], ALU.mult, ALU.add,
                          reads=[ps, tap, uk(ti)], writes=[uk(ti)])
                th, sg0, sg1 = R["th"], R["sg0"], R["sg1"]
                P.act(th[:], u[:, 6, :], AF.Tanh, reads=[uk(6)], writes=[th])
                P.act(sg0[:], u[:, 8, :], AF.Sigmoid, reads=[uk(8)], writes=[sg0])
                P.act(sg1[0:32, :], u[0:32, 9, :], AF.Sigmoid, reads=[uk(9)], writes=[sg1])
                for ct in range(2):
                    ps = npb()
                    P.mm(ps[:, 0:n], g2a[:, ct * 128:(ct + 1) * 128], sg0[:], start=True, stop=False, reads=[g2a, sg0], writes=[ps])
                    P.mm(ps[:, 0:n], g2b[0:32, ct * 128:(ct + 1) * 128], sg1[0:32, :], start=False, stop=True, reads=[g2b, sg1], writes=[ps])
                    gt = gate_t[ct]
                    P.copy(gt[:], ps[:, 0:n], reads=[ps], writes=[gt], eng="act")
                    P.dma(rw_gate[ct, :, g0:g0 + n], gt[:], reads=[gt], q="pool")
                    P.dma(rw_vT[ct, :, g0:g0 + n], u[:, 4 + ct, :], reads=[uk(4 + ct)], q="pool")
                for j in range(2):
                    ps = npb()
                    for ct in range(2):
                        P.tr(ps[:, ct * 128:(ct + 1) * 128], u[:, 4 + ct, j * 128:(j + 1) * 128], ident[:], reads=[uk(4 + ct), ident], writes=[ps])
                    P.copy(vt[:, j, :], ps[:, 0:256], reads=[ps], writes=[vt], eng="act")
                P.dma(rw_v[g0:g0 + n, :].rearrange("(j p) c -> p j c", p=128), vt[:], reads=[vt], q="pool")
                def chain(d, ct, R, fmo, tmo, u=u, uk=uk, g0=g0, n=n):
                    if True:
                        ukk = uk(2 + ct); urk = uk(ct)
                        u_k = u[:, 2 + ct, :]; u_r = u[:, ct, :]
                        lw, asg, kkr, sq, nr, kk, b_, t_, kd = (R[k] for k in ("lw", "asg", "kkr", "sq", "nr", "kk", "b", "t", "kd"))
                        cs_, cl, clx, e1, e2, e3, bh, kh, tz = (R[k] for k in ("cs", "cl", "clx", "e1", "e2", "e3", "bh", "kh", "tz"))
                        ps = npb()
                        P.mm(ps[:, 0:n], w2s[d * 64:(d + 1) * 64, ct * 128:(ct + 1) * 128], th[d * 64:(d + 1) * 64, :], reads=[w2s, th], writes=[ps])
                        yield
                        P.act(lw[:], ps[:, 0:n], AF.Sigmoid, reads=[ps, vec], writes=[lw], bias=vec[:, 0, d, ct:ct + 1])
                        yield
                        P.ts(lw[:], lw[:], DECAY_C, None, ALU.mult, reads=[lw], writes=[lw])
                        yield
                        ps = npb()
                        P.mm(ps[:, 0:n], a2s[d * 64:(d + 1) * 64, ct * 128:(ct + 1) * 128], u[d * 64:(d + 1) * 64, 7, :], reads=[a2s, uk(7)], writes=[ps])
                        yield
                        P.act(asg[:], ps[:, 0:n], AF.Sigmoid, reads=[ps, vec], writes=[asg], bias=vec[:, 1, d, ct:ct + 1])
                        yield
                        P.ts(kkr[:], u_k, vec[:, 2, d, ct:ct + 1], None, ALU.mult, reads=[ukk, vec], writes=[kkr])
                        yield
                        P.tt(sq[:], kkr[:], kkr[:], ALU.mult, reads=[kkr], writes=[sq])
                        yield
                        ps = npb()
                        P.mm(ps[:, 0:n], bd64[:], sq[:], reads=[bd64, sq], writes=[ps])
                        yield
                        P.act(nr[:], ps[:, 0:n], AF.Sqrt, reads=[ps], writes=[nr])
                        yield
                        P.ts(nr[:], nr[:], 1e-12, None, ALU.max, reads=[nr], writes=[nr])
                        yield
                        P.recip(nr[:], nr[:], reads=[nr], writes=[nr])
                        yield
                        P.tt(kk[:], kkr[:], nr[:], ALU.mult, reads=[kkr, nr], writes=[kk])
                        yield
                        P.tt(b_[:], kk[:], asg[:], ALU.mult, reads=[kk, asg], writes=[b_])
                        yield
                        P.ts(t_[:], asg[:], -1.0, vec[:, 3, d, ct:ct + 1], ALU.add, ALU.mult, reads=[asg, vec], writes=[t_])
                        yield
                        P.stt(kd[:], t_[:], 1.0, u_k, ALU.add, ALU.mult, reads=[t_, ukk], writes=[kd])
                        yield
                        for j in range(2):
                            sl = slice(j * 128, (j + 1) * 128)
                            P.op("dve", lambda e, sl=sl: e.tensor_tensor_scan(cs_[:, sl], ones_f[:, 0:128], lw[:, sl], 0.0, ALU.mult, ALU.add),
                                 reads=[ones_f, lw], writes=[cs_])
                            yield
                        if d == 0:
                            clt = cs_
                        else:
                            for j in range(2):
                                sl = slice(j * 128, (j + 1) * 128)
                                P.ts(cl[:, sl], cs_[:, sl], -1.0, cs_[:, j * 128 + 127:j * 128 + 128], ALU.mult, ALU.add, reads=[cs_], writes=[cl])
                                yield
                            P.tt(cl[:], cl[:], lw[:], ALU.add, reads=[cl, lw], writes=[cl])
                            yield
                            clt = cl
                        P.tt(clx[:], clt[:], lw[:], ALU.subtract, reads=[clt, lw], writes=[clx])
                        yield
                        P.act(e1[:], clt[:], AF.Exp, reads=[clt], writes=[e1])
                        yield
                        P.act(e2[:], clt[:], AF.Exp, reads=[clt], writes=[e2], scale=-1.0)
                        yield
                        P.act(e3[:], clx[:], AF.Exp, reads=[clx], writes=[e3])
                        yield
                        for j in range(2):
                            cidx = g0 // 128 + j
                            col = j * 128 + (127 if d == 0 else 0)
                            P.copy(pcs[:, d, ct, cidx:cidx + 1], e1[:, col:col + 1], reads=[e1], writes=[pcs])
                            yield
                        P.stt(fmo[:, 0, ct, :], kk[:], -1.0, e3[:], ALU.mult, ALU.mult, reads=[kk, e3], writes=[fmo])
                        yield
                        P.tt(fmo[:, 1, ct, :], b_[:], e2[:], ALU.mult, reads=[b_, e2], writes=[fmo])
                        yield
                        P.tt(fmo[:, 2, ct, :], kd[:], e2[:], ALU.mult, reads=[kd, e2], writes=[fmo])
                        yield
                        P.tt(fmo[:, 3, ct, :], u_r, e1[:], ALU.mult, reads=[urk, e1], writes=[fmo])
                        yield
                        for j in range(2):
                            sl = slice(j * 128, (j + 1) * 128)
                            cidx = g0 // 128 + j
                            P.ts(bh[:, sl], fmo[:, 1, ct, sl], pcs[:, d, ct, cidx:cidx + 1], None, ALU.mult, reads=[fmo, pcs], writes=[bh])
                            yield
                            P.ts(kh[:, sl], fmo[:, 2, ct, sl], pcs[:, d, ct, cidx:cidx + 1], None, ALU.mult, reads=[fmo, pcs], writes=[kh])
                            yield
                        psT = npb()
                        for a, src in ((0, bh), (1, kh)):
                            for j in range(2):
                                P.tr(psT[:, (a * 2 + j) * 128:(a * 2 + j + 1) * 128], src[:, j * 128:(j + 1) * 128], ident[:],
                                     reads=[src, ident], writes=[psT])
                                yield
                        P.copy(tmo[:, :, :, ct * 128:(ct + 1) * 128], psT[:, :].rearrange("p (a j c) -> p j a c", a=2, j=2),
                               reads=[psT], writes=[tmo], eng="act")
                        yield
                        if d == 0:
                            P.stt(zz[:, ct, :], kd[:], vec[:, 4, d, ct:ct + 1], u_r, ALU.mult, ALU.mult, reads=[kd, vec, urk], writes=[zz])
                            yield
                        else:
                            P.stt(tz[:], kd[:], vec[:, 4, d, ct:ct + 1], u_r, ALU.mult, ALU.mult, reads=[kd, vec, urk], writes=[tz])
                            yield
                            P.tt(zz[:, ct, :], zz[:, ct, :], tz[:], ALU.add, reads=[zz, tz], writes=[zz])
                            yield
                for d in range(2):
                    fmo = fmout[d]; tmo = tmout[d]
                    gens = [chain(d, ct, Rs[ct], fmo, tmo) for ct in range(2)]
                    while gens:
                        alive = []
                        for g_ in gens:
                            try:
                                next(g_)
                                alive.append(g_)
                            except StopIteration:
                                pass
                        gens = alive
                    P.dma(rw_fm[d].rearrange("a c p t -> (a c) p t")[:, :, g0:g0 + n].rearrange("q p t -> p q t"),
                          fmo[:].rearrange("p a c t -> p (a c) t"), reads=[fmo], q="pool")
                    for a in range(2):
                        P.dma(rw_tm[d][a, g0:g0 + n, :].rearrange("(j p) c -> p j c", p=128), tmo[:, :, a, :], reads=[tmo], q="pool")
                P.dma(rw_zz[:, :, g0:g0 + n].rearrange("c p t -> p c t"), zz[:], reads=[zz], q="pool")

    def phase3(L, want_ctx):
        with P.phase():
            tri = P.sb([128, 4, 128])
            P.dma(tri[:], c_tri.rearrange("a p c -> p a c"), writes=[tri])
            LT, LE, GT, GE = (tri[:, i, :] for i in range(4))
            lnv = P.sb([128, 2, 2]); gne = P.sb([128, 1])
            P.dma(lnv[:], rw_ln[L], writes=[lnv])
            P.memset(gne[:], GN_EPS, writes=[gne])
            H = [P.sb([128, 2, 64]) for _ in range(2)]
            for d in range(2):
                P.memset(H[d][:].rearrange("p a b -> p (a b)"), 0.0, writes=[H[d]])
            yacc = P.sb([128, 2, T])
            fm = [P.sb([128, 4, 2, 128]) for _ in range(4)]
            tm = [P.sb([128, 2, 256]) for _ in range(4)]
            vv = [P.sb([128, 256]) for _ in range(4)]
            NB = 8
            W = {k: [P.sb([128, 128]) for _ in range(NB)] for k in ("N", "A", "Kt", "Rb", "Rk", "M", "N2", "A2", "Nd", "Ad", "MT")}
            W["Tt"] = [P.sb([128, 256]) for _ in range(NB)]
            blk = P.sb([128, 4, 128])
            P.dma(blk[:], c_blk.rearrange("a p c -> p a c"), writes=[blk])
            BD16, QS16, QS32, QS64 = (blk[:, i, :] for i in range(4))
            rhs_t = [P.sb([128, 64]) for _ in range(NB)]
            u_t = [P.sb([128, 64]) for _ in range(NB)]
            bank_pre = pb[0:2]; bank_sq = pb[2:4]; bank_ch = pb[5:7]; ybank = [pb[7], pb[4]]
            written = set()
            order = {0: list(range(NKT)), 1: [1, 0] + list(range(NKT - 1, 1, -1))}
            def unit(d, c, us, f_, t_, v_, ct, hh, ypair, mS, mA, mI, yc0):
                    h = ct * 2 + hh
                    p0 = hh * 64
                    at = f_[p0:p0 + 64, 0, ct, :]; bt = f_[p0:p0 + 64, 1, ct, :]
                    kt_ = f_[p0:p0 + 64, 2, ct, :]; rt = f_[p0:p0 + 64, 3, ct, :]
                    bh = t_[:, 0, h * 64:(h + 1) * 64]; kh = t_[:, 1, h * 64:(h + 1) * 64]
                    vh = v_[:, h * 64:(h + 1) * 64]
                    i_ = us
                    Nn, Aa, Kt, Rb, Rk, Mm = (W[k][i_ % NB] for k in ("N", "A", "Kt", "Rb", "Rk", "M"))
                    N2 = W["N2"][i_ % NB]; A2 = W["A2"][i_ % NB]
                    rh = rhs_t[i_ % NB]; uu = u_t[i_ % NB]
                    bp = nxt("bpre", bank_pre)
                    P.mm(bp[:, 0:128], bt, at, reads=[f_], writes=[bp])
                    P.mm(bp[:, 128:256], at, bt, reads=[f_], writes=[bp])
                    P.mm(bp[:, 256:384], kt_, at, reads=[f_], writes=[bp])
                    P.mm(bp[:, 384:512], bt, rt, reads=[f_], writes=[bp])
                    P.tt(Nn[:], bp[:, 0:128], mS, ALU.mult, reads=[bp, tri], writes=[Nn])
                    P.tt(Aa[:], bp[:, 128:256], mA, ALU.mult, reads=[bp, tri], writes=[Aa])
                    P.tt(Kt[:], bp[:, 256:384], mS, ALU.mult, reads=[bp, tri], writes=[Kt])
                    P.tt(Rb[:], bp[:, 384:512], mI, ALU.mult, reads=[bp, tri], writes=[Rb])
                    bp2 = nxt("bpre", bank_pre)
                    P.mm(bp2[:, 0:128], kt_, rt, reads=[f_], writes=[bp2])
                    P.tt(Rk[:], bp2[:, 0:128], mI, ALU.mult, reads=[bp2, tri], writes=[Rk])
                    Nd, Ad, MT, Tt = (W[k][i_ % NB] for k in ("Nd", "Ad", "MT", "Tt"))
                    P.tt(Nd[:], Nn[:], BD16, ALU.mult, reads=[Nn, blk], writes=[Nd])
                    P.tt(Ad[:], Aa[:], BD16, ALU.mult, reads=[Aa, blk], writes=[Ad])
                    P.tt(Mm[:], Nd[:], ident[:], ALU.add, reads=[Nd, ident], writes=[Mm])
                    P.tt(MT[:], Ad[:], ident[:], ALU.add, reads=[Ad, ident], writes=[MT])
                    yield
                    curN, curA, nxtN, nxtA = Nd, Ad, N2, A2
                    for lev in range(1, 4):
                        bs = nxt("bsq", bank_sq)
                        P.mm(bs[:, 0:128], curN[:], curA[:], reads=[curN, curA], writes=[bs])
                        P.mm(bs[:, 128:256], curA[:], curN[:], reads=[curN, curA], writes=[bs])
                        P.copy(nxtA[:], bs[:, 0:128], reads=[bs], writes=[nxtA], eng="act")
                        P.copy(nxtN[:], bs[:, 128:256], reads=[bs], writes=[nxtN], eng="act")
                        yield
                        bs2 = nxt("bsq", bank_sq)
                        P.mm(bs2[:, 0:128], nxtA[:], Mm[:], reads=[nxtA, Mm], writes=[bs2])
                        P.mm(bs2[:, 128:256], nxtN[:], MT[:], reads=[nxtN, MT], writes=[bs2])
                        P.tt(Mm[:], Mm[:], bs2[:, 0:128], ALU.add, reads=[Mm, bs2], writes=[Mm])
                        P.tt(MT[:], MT[:], bs2[:, 128:256], ALU.add, reads=[MT, bs2], writes=[MT])
                        curN, curA, nxtN, nxtA = nxtN, nxtA, curN, curA
                        yield
                    for qi, QS in enumerate((QS16, QS32, QS64)):
                        last = (qi == 2)
                        NQ, AQ = curN, curA
                        P.tt(AQ[:], Aa[:], QS, ALU.mult, reads=[Aa, blk], writes=[AQ])
                        if not last:
                            P.tt(NQ[:], Nn[:], QS, ALU.mult, reads=[Nn, blk], writes=[NQ])
                        bs = nxt("bsq", bank_sq)
                        P.mm(bs[:, 0:128], AQ[:], Mm[:], reads=[AQ, Mm], writes=[bs])
                        if not last:
                            P.mm(bs[:, 128:256], NQ[:], MT[:], reads=[NQ, MT], writes=[bs])
                        P.copy(Tt[:, 0:128], bs[:, 0:128], reads=[bs], writes=[Tt], eng="act")
                        if not last:
                            P.copy(Tt[:, 128:256], bs[:, 128:256], reads=[bs], writes=[Tt], eng="act")
                        yield
                        bs2 = nxt("bsq", bank_sq)
                        P.mm(bs2[:, 0:128], MT[:], Tt[:, 0:128], reads=[MT, Tt], writes=[bs2])
                        if not last:
                            P.mm(bs2[:, 128:256], Mm[:], Tt[:, 128:256], reads=[Mm, Tt], writes=[bs2])
                        P.tt(Mm[:], Mm[:], bs2[:, 0:128], ALU.add, reads=[Mm, bs2], writes=[Mm])
                        if not last:
                            P.tt(MT[:], MT[:], bs2[:, 128:256], ALU.add, reads=[MT, bs2], writes=[MT])
                        yield
                    Hh = H[d][p0:p0 + 64, ct, :]
                    bc = nxt("bch", bank_ch)
                    P.mm(bc[:, 0:64], at, Hh, start=True, stop=False, reads=[f_, H[d]], writes=[bc])
                    P.mm(bc[:, 0:64], Kt[:], vh, start=False, stop=True, reads=[Kt, v_], writes=[bc])
                    P.copy(rh[:], bc[:, 0:64], reads=[bc], writes=[rh], eng="act")
                    yield
                    P.mm(bc[:, 64:128], Mm[:], rh[:], reads=[Mm, rh], writes=[bc])
                    P.copy(uu[:], bc[:, 64:128], reads=[bc], writes=[uu], eng="act")
                    yield
                    P.mm(ypair[p0:p0 + 64, yc0:yc0 + 128], Hh, rt, start=True, stop=False, reads=[H[d], f_], writes=[ypair])
                    P.mm(ypair[p0:p0 + 64, yc0:yc0 + 128], uu[:], Rb[:], start=False, stop=False, reads=[uu, Rb], writes=[ypair])
                    P.mm(ypair[p0:p0 + 64, yc0:yc0 + 128], vh, Rk[:], start=False, stop=True, reads=[v_, Rk], writes=[ypair])
                    P.mm(bc[p0:p0 + 64, 128:192], bh, uu[:], start=True, stop=False, reads=[t_, uu], writes=[bc])
                    P.mm(bc[p0:p0 + 64, 128:192], kh, vh, start=False, stop=True, reads=[t_, v_], writes=[bc])
                    P.stt(Hh, Hh, pcs[p0:p0 + 64, d, ct, c:c + 1], bc[p0:p0 + 64, 128:192], ALU.mult, ALU.add,
                          reads=[H[d], pcs, bc], writes=[H[d]])
                    yield

            def drive(gens, lock):
                if not lock:
                    for g_ in gens:
                        for _ in g_:
                            pass
                    return
                while gens:
                    alive = []
                    for g_ in gens:
                        try:
                            next(g_)
                            alive.append(g_)
                        except StopIteration:
                            pass
                    gens = alive

            for step in range(NKT):
                gens = []
                cs_ = {}
                for d in range(2):
                    c = order[d][step]
                    cs_[d] = c
                    g = c * 128
                    f_ = nxt("fm3", fm); t_ = nxt("tm3", tm); v_ = nxt("vv3", vv)
                    P.dma(f_[:].rearrange("p a c t -> p (a c) t"),
                          rw_fm[d].rearrange("a c p t -> (a c) p t")[:, :, g:g + 128].rearrange("q p t -> p q t"), writes=[f_])
                    P.dma(t_[:], rw_tm[d][:, g:g + 128, :].rearrange("a t c -> t a c"), writes=[t_])
                    P.dma(v_[:], rw_v[g:g + 128, :], writes=[v_])
                    mS, mA, mI = (LT, GT, LE) if d == 0 else (GT, LT, GE)
                    for ct in range(2):
                        for hh in range(2):
                            gens.append(unit(d, c, d * 4 + ct * 2 + hh, f_, t_, v_, ct, hh, ybank[hh], mS, mA, mI, (d * 2 + ct) * 128))
                    if RW_LOCK < 8:
                        drive(gens, RW_LOCK > 1)
                        gens = []
                if gens:
                    drive(gens, True)
                for d in range(2):
                    c = cs_[d]
                    g = c * 128
                    for ct in range(2):
                        yc0 = (d * 2 + ct) * 128
                        first = (ct, c) not in written
                        written.add((ct, c))
                        for hh in range(2):
                            p0 = hh * 64
                            ysl = yacc[p0:p0 + 64, ct, g:g + 128]
                            yk_ = f"yacc{ct}_{hh}"
                            if first:
                                P.copy(ysl, ybank[hh][p0:p0 + 64, yc0:yc0 + 128], reads=[ybank[hh]], writes=[yk_], eng="act")
                            else:
                                P.tt(ysl, ysl, ybank[hh][p0:p0 + 64, yc0:yc0 + 128], ALU.add, reads=[yk_, ybank[hh]], writes=[yk_])
            if f"rw{L}" in dbg:
                dump(f"rwH0_{L}", H[0][:], [128, 2, 64], [H[0]])
                dump(f"rwH1_{L}", H[1][:], [128, 2, 64], [H[1]])
                dump(f"rwpcs_{L}", pcs[:], [128, 2, 2, NKT], [pcs])
                dump(f"rwyacc_{L}", yacc[:], [128, 2, T], [yacc])
                for d in range(2):
                    tdb = P.sb([128, 8, 512])
                    P.dma(tdb[:], rw_fm[d].rearrange("a c p t -> (a c) p t")[:, :, 0:512].rearrange("q p t -> p q t"), writes=[tdb])
                    dump(f"rwfm{d}_{L}", tdb[:], [128, 8, 512], [tdb])
            gt_ = [P.sb([128, 512]) for _ in range(2)]; vT_ = [P.sb([128, 512]) for _ in range(2)]; zz_ = [P.sb([128, 512]) for _ in range(2)]
            w1 = [P.sb([128, 512]) for _ in range(2)]; w2_ = [P.sb([128, 512]) for _ in range(2)]; w3 = [P.sb([128, 512]) for _ in range(2)]
            ob3 = [P.sb([128, 512], BF16) for _ in range(2)]
            for bi, (g0, n) in enumerate(BLOCKS):
                if g0 < T_CTX and not want_ctx:
                    continue
                for ct in range(2):
                    y = yacc[:, ct, g0:g0 + n]
                    g_ = nxt("gt_", gt_); v2 = nxt("vT_", vT_); z2 = nxt("zz_", zz_)
                    P.dma(g_[:, 0:n], rw_gate[ct, :, g0:g0 + n], writes=[g_])
                    P.dma(v2[:, 0:n], rw_vT[ct, :, g0:g0 + n], writes=[v2])
                    P.dma(z2[:, 0:n], rw_zz[ct, :, g0:g0 + n], writes=[z2])
                    ps = npb()
                    P.mm(ps[:, 0:n], bd64[:], y, reads=[bd64, f"yacc{ct}_0", f"yacc{ct}_1"], writes=[ps])
                    yc = nxt("w1", w1)
                    P.stt(yc[:, 0:n], ps[:, 0:n], -1.0 / 64, y, ALU.mult, ALU.add, reads=[ps, f"yacc{ct}_0", f"yacc{ct}_1"], writes=[yc])
                    sq = nxt("w2_", w2_)
                    P.tt(sq[:, 0:n], yc[:, 0:n], yc[:, 0:n], ALU.mult, reads=[yc], writes=[sq], eng="pool")
                    ps2 = npb()
                    P.mm(ps2[:, 0:n], bd64[:], sq[:, 0:n], reads=[bd64, sq], writes=[ps2])
                    P.act(sq[:, 0:n], ps2[:, 0:n], AF.Sqrt, reads=[ps2, gne], writes=[sq], scale=1.0 / 64, bias=gne[:, 0:1])
                    P.recip(sq[:, 0:n], sq[:, 0:n], reads=[sq], writes=[sq])
                    P.tt(yc[:, 0:n], yc[:, 0:n], sq[:, 0:n], ALU.mult, reads=[yc, sq], writes=[yc])
                    o1 = nxt("w3", w3)
                    P.act(o1[:, 0:n], yc[:, 0:n], AF.Identity, reads=[yc, lnv], writes=[o1], scale=lnv[:, 0, ct:ct + 1], bias=lnv[:, 1, ct:ct + 1])
                    ps3 = npb()
                    P.mm(ps3[:, 0:n], bd64[:], z2[:, 0:n], reads=[bd64, z2], writes=[ps3])
                    P.tt(v2[:, 0:n], ps3[:, 0:n], v2[:, 0:n], ALU.mult, reads=[ps3, v2], writes=[v2])
                    P.tt(o1[:, 0:n], o1[:, 0:n], v2[:, 0:n], ALU.add, reads=[o1, v2], writes=[o1], eng="pool")
                    o = nxt("ob3", ob3)
                    P.tt(o[:, 0:n], o1[:, 0:n], g_[:, 0:n], ALU.mult, reads=[o1, g_], writes=[o])
                    P.dma(mixT[6 + ct, :, g0:g0 + n], o[:, 0:n], reads=[o], q="pool")

    def phase4(L, want_ctx):
        with P.phase():
            T_ = {"big": [P.sb([128, 8, 512]) for _ in range(5)],
                  "tmp": [P.sb([128, 512]) for _ in range(4)],
                  "rs": [P.sb([128, 512]) for _ in range(2)]}
            wo = P.sb([128, 8, 1024], BF16)
            mb = [P.sb([128, 8, 512], BF16) for _ in range(2)]
            hb = [P.sb([128, 8, 512], BF16) for _ in range(2)]
            load_w_bf16(wo[:], wo, w_out[L], D, D, T_)
            set_norm_params(2)
            for bi, (g0, n) in enumerate(BLOCKS):
                if g0 < T_CTX and not want_ctx:
                    continue
                s = 1 if g0 < T_CTX else 0
                m_ = nxt("mb", mb)
                P.dma(m_[:, :, 0:n], mixT[:, :, g0:g0 + n].rearrange("a p t -> p a t"), writes=[m_])
                xb = nxt("big", T_["big"])
                P.dma(xb[:, :, 0:n], xT[:, :, g0:g0 + n].rearrange("a p t -> p a t"), writes=[xb])
                x1 = nxt("big", T_["big"])
                for ft in range(8):
                    ps = npb()
                    for kt in range(8):
                        P.mm(ps[:, 0:n], wo[:, kt, ft * 128:(ft + 1) * 128], m_[:, kt, 0:n], start=(kt == 0), stop=(kt == 7),
                             reads=[wo, m_], writes=[ps])
                    P.stt(x1[:, ft, 0:n], ps[:, 0:n], modT[:, 16 + ft, s:s + 1], xb[:, ft, 0:n], ALU.mult, ALU.add,
                          reads=[ps, modT, xb], writes=[x1])
                P.dma(xT[:, :, g0:g0 + n].rearrange("a p t -> p a t"), x1[:, :, 0:n], reads=[x1], q="pool")
                h_ = nxt("hb", hb)
                norm_block(x1, n, (lambda ft, h_=h_, n=n: (h_[:, ft, 0:n], [h_])),
                           (lambda ft, s=s: gs[:, ft, s:s + 1]), (lambda ft, s=s: sh[:, ft, s:s + 1]), T_)
                P.dma(h2T[:, :, g0:g0 + n].rearrange("a p t -> p a t"), h_[:, :, 0:n], reads=[h_], q="pool")

    def phase5(L, want_ctx):
        for hf in range(2):
            with P.phase():
                T_ = {"big": [P.sb([128, 8, 512]) for _ in range(4)]}
                f1 = P.sb([128, 8, 2048], BF16)
                f2 = P.sb([128, 16, 1024], BF16)
                hb = [P.sb([128, 8, 512], BF16) for _ in range(2)]
                ab = [P.sb([128, 16, 512], BF16) for _ in range(2)]
                rt_ = [P.sb([128, 512], BF16) for _ in range(3)]
                load_w_bf16(f1[:], f1, w_fc1[L][:, hf * 2048:(hf + 1) * 2048], D, 2048, T_)
                load_w_bf16(f2[:], f2, w_fc2[L][hf * 2048:(hf + 1) * 2048, :], 2048, D, T_)
                for bi, (g0, n) in enumerate(BLOCKS):
                    if g0 < T_CTX and not want_ctx:
                        continue
                    s = 1 if g0 < T_CTX else 0
                    h_ = nxt("hb5", hb)
                    P.dma(h_[:, :, 0:n], h2T[:, :, g0:g0 + n].rearrange("a p t -> p a t"), writes=[h_])
                    xb = nxt("big", T_["big"])
                    P.dma(xb[:, :, 0:n], xT[:, :, g0:g0 + n].rearrange("a p t -> p a t"), writes=[xb])
                    a_ = nxt("ab", ab)
                    for ht in range(16):
                        ps = npb()
                        for kt in range(8):
                            P.mm(ps[:, 0:n], f1[:, kt, ht * 128:(ht + 1) * 128], h_[:, kt, 0:n], start=(kt == 0), stop=(kt == 7),
                                 reads=[f1, h_], writes=[ps])
                        r_ = nxt("rt", rt_)
                        P.act(r_[:, 0:n], ps[:, 0:n], AF.Relu, reads=[ps], writes=[r_])
                        P.tt(a_[:, ht, 0:n], r_[:, 0:n], r_[:, 0:n], ALU.mult, reads=[r_], writes=[a_], eng=("dve" if ht % 2 else "pool"))
                    for ft in range(8):
                        ps = npb()
                        for ht in range(16):
                            P.mm(ps[:, 0:n], f2[:, ht, ft * 128:(ft + 1) * 128], a_[:, ht, 0:n], start=(ht == 0), stop=(ht == 15),
                                 reads=[f2, a_], writes=[ps])
                        P.stt(xb[:, ft, 0:n], ps[:, 0:n], modT[:, 40 + ft, s:s + 1], xb[:, ft, 0:n], ALU.mult, ALU.add,
                              reads=[ps, modT, xb], writes=[xb])
                    P.dma(xT[:, :, g0:g0 + n].rearrange("a p t -> p a t"), xb[:, :, 0:n], reads=[xb], q="pool")

    def stage_final():
        with P.phase():
            T_ = {"big": [P.sb([128, 8, 512]) for _ in range(5)],
                  "tmp": [P.sb([128, 512]) for _ in range(4)],
                  "rs": [P.sb([128, 512]) for _ in range(2)]}
            yo = [P.sb([128, 4, 1024]) for _ in range(2)]
            for bi, (g0, n) in enumerate(BLOCKS):
                if g0 < T_CTX:
                    continue
                xb = nxt("big", T_["big"])
                P.dma(xb[:, :, 0:n], xT[:, :, g0:g0 + n].rearrange("a p t -> p a t"), writes=[xb])
                yb = nxt("big", T_["big"])
                norm_block(xb, n, (lambda ft, yb=yb, n=n: (yb[:, ft, 0:n], [yb])), (lambda ft: fg[:, ft:ft + 1]), (lambda ft: None), T_)
                y_ = nxt("yo", yo)
                for a in range(4):
                    for f2_ in range(2):
                        ps = npb()
                        for q4 in range(4):
                            ft = f2_ * 4 + q4
                            P.tr(ps[:, q4 * 128:(q4 + 1) * 128], yb[:, ft, a * 128:(a + 1) * 128], ident[:], reads=[yb, ident], writes=[ps])
                        P.copy(y_[:, a, f2_ * 512:(f2_ + 1) * 512], ps[:, :], reads=[ps], writes=[y_], eng=("act" if f2_ else "dve"))
                t0 = g0 - T_CTX
                P.dma(y_out[t0:t0 + n, :].rearrange("(a p) f -> p a f", p=128), y_[:], reads=[y_], q="pool")

    stage_A()
    for L in range(nl):
        want_ctx = (L < DEPTH - 1) if force_ctx is None else force_ctx
        phase1(L)
        phase2(L, want_ctx)
        if "rw" in mixers:
            phase3(L, want_ctx)
        if "mixT" in dbg or f"mixT{L}" in dbg:
            with P.phase():
                t_ = P.sb([128, 8, T], BF16)
                P.dma(t_[:], mixT.rearrange("a p t -> p a t"), writes=[t_])
                dump("mixT" if "mixT" in dbg else f"mixT{L}", t_[:], [128, 8, T], [t_], BF16)
        if f"xT{L}" in dbg:
            with P.phase():
                t_ = P.sb([128, 8, T])
                P.dma(t_[:], xT.rearrange("a p t -> p a t"), writes=[t_])
                dump(f"xT{L}", t_[:], [128, 8, T], [t_])
        if "rw" not in mixers:
            with P.phase():
                z_ = P.sb([128, 2, T], BF16)
                P.memset(z_[:].rearrange("p a t -> p (a t)"), 0.0, writes=[z_])
                P.dma(mixT[6:8].rearrange("a p t -> p a t"), z_[:], reads=[z_], q="pool")
        phase4(L, want_ctx)
        if do_mlp:
            phase5(L, want_ctx)
    if "xT" in dbg:
        with P.phase():
            t_ = P.sb([128, 8, T])
            P.dma(t_[:], xT.rearrange("a p t -> p a t"), writes=[t_])
            dump("xT", t_[:], [128, 8, T], [t_])
    stage_final()
    P.emit()
    P.close()
    return nc


_CONSTS = None


def consts():
    global _CONSTS
    if _CONSTS is None:
        c64, s64 = rope_tables(64)
        c32, s32 = rope_tables(32)
        c96 = np.ones((96, T), np.float32); s96 = np.zeros((96, T), np.float32)
        c96[64:] = c32; s96[64:] = s32
        bd = np.zeros((128, 128), np.float32); bd[:64, :64] = 1; bd[64:, 64:] = 1
        _CONSTS = {
            "c_ident": np.eye(128, dtype=np.float32), "c_bd64": bd,
            "c_rt128": np.ascontiguousarray(np.kron(np.eye(2, dtype=np.float32), rot_lhsT(64))),
            "c_rt96": rot_lhsT(32, off=64, size=96),
            "c_cos128": np.ascontiguousarray(np.tile(c64, (2, 1))), "c_sin128": np.ascontiguousarray(np.tile(s64, (2, 1))),
            "c_cos96": c96, "c_sin96": s96,
            "c_tri": np.stack([np.triu(np.ones((128, 128), np.float32), 1), np.triu(np.ones((128, 128), np.float32), 0),
                               np.tril(np.ones((128, 128), np.float32), -1), np.tril(np.ones((128, 128), np.float32), 0)], 0),
            "c_blk": (lambda bd: np.stack([bd(16), bd(32) - bd(16), bd(64) - bd(32), 1.0 - bd(64)], 0).astype(np.float32))(
                lambda b: np.kron(np.eye(128 // b), np.ones((b, b)))),
            "na_masks": np.ascontiguousarray(NA_MASKS).astype(ml_dtypes.bfloat16),
        }
    return _CONSTS


def prep_shared(inp):
    f = np.float32
    L = DEPTH
    w_in = np.asarray(inp["w_in"], f)
    kr96 = np.zeros((L, D, 96), f)
    kr96[:, :, 64:] = w_in[:, :, MLA_C0 + 384:MLA_C0 + 416]
    wukv = np.asarray(inp["mla_w_ukv"], f).reshape(L, 128, 4, 2, 64)
    wukv = np.ascontiguousarray(np.concatenate([wukv[:, :, :, 0, :].reshape(L, 128, 256), wukv[:, :, :, 1, :].reshape(L, 128, 256)], -1))
    sh = dict(consts())
    sh.update({
        "w_mod": np.asarray(inp["w_mod"], f), "b_modT": featT(inp["b_mod"]), "n1g": featT(inp["norm1_g"]),
        "n2g": featT(inp["norm2_g"]), "fng": featT(inp["final_norm_g"]), "w_in": w_in, "w_kr96": kr96,
        "gqa_qn": np.tile(np.asarray(inp["gqa_q_norm"], f), (1, 2))[:, :, None].copy(), "gqa_kn": np.tile(np.asarray(inp["gqa_k_norm"], f), (1, 2))[:, :, None].copy(),
        "mla_qn": featT(inp["mla_q_norm"]), "mla_kvn": featT(inp["mla_kv_norm"]),
        "w_uq": np.asarray(inp["mla_w_uq"], f), "w_ukv": wukv,
        "na_tb": na_bias_table(np.asarray(inp["na_rpb"], f)), "na_rpbf": np.asarray(inp["na_rpb"], f).reshape(L, 4, 1, 465),
        "rw_tap": np.ascontiguousarray(np.pad(np.asarray(inp["rwkv_shift"], f), ((0, 0), (0, 0), (0, 96))).reshape(L, 3, 10, 128).transpose(0, 3, 2, 1)),
        "rw_vec": np.ascontiguousarray(np.stack([np.asarray(inp[k], f).reshape(L, 2, 2, 128) for k in
                                                 ("rwkv_w0", "rwkv_a0", "rwkv_k_k", "rwkv_k_a", "rwkv_r_k")], 1).transpose(0, 4, 1, 2, 3)),
        "rw_w2": np.asarray(inp["rwkv_w2"], f).reshape(L, 128, 256), "rw_a2": np.asarray(inp["rwkv_a2"], f).reshape(L, 128, 256),
        "rw_g2": np.asarray(inp["rwkv_g2"], f),
        "rw_ln": np.ascontiguousarray(np.stack([np.asarray(inp[k], f).reshape(L, 2, 128) for k in ("rwkv_lnx_w", "rwkv_lnx_b")], 1).transpose(0, 3, 1, 2)),
        "w_out": np.asarray(inp["w_out"], f), "w_fc1": np.asarray(inp["w_fc1"], f), "w_fc2": np.asarray(inp["w_fc2"], f),
    })
    return sh


def prep_core(inp, b, shared):
    m = dict(shared)
    m["x"] = np.ascontiguousarray(inp["x"][b], np.float32)
    m["ctx"] = np.ascontiguousarray(inp["ctx"][b], np.float32)
    cv = np.stack([featT(inp["c"][b]), featT(inp["c_ctx"])], -1).astype(np.float32)
    m["cvec"] = np.ascontiguousarray(cv)
    return m


def kernel(**inputs):
    nc = build()
    shared = prep_shared(inputs)
    in_maps = [prep_core(inputs, b, shared) for b in range(8)]
    res = run_bass_kernel_spmd(nc, in_maps, core_ids=list(range(8)))
    return np.stack([r["y"] for r in res.results], 0).astype(np.float32)
```

```python
import numpy as np
import ml_dtypes
from contextlib import ExitStack
import concourse.bass as bass
import concourse.mybir as mybir
from concourse.bass_utils import run_bass_kernel_spmd

F32 = mybir.dt.float32
BF16 = mybir.dt.bfloat16
AF = mybir.ActivationFunctionType
ALU = mybir.AluOpType
AX = mybir.AxisListType

D = 1024
T_CTX = 256
T_LAT = 4096
T = T_CTX + T_LAT
GRID_W = 64
DEPTH = 4
NA_C0, GQA_C0, MLA_C0, RW_C0 = 0, 768, 1280, 1696
IN_COLS = 2880
RMS_EPS = 1e-6
GN_EPS = 64e-5
NEG = -30000.0
BLOCKS = [(0, 256)] + [(256 + 512 * i, 512) for i in range(8)]
NKT = T // 128


def hc(g):
    return g + 1 if g < T_CTX else g + 3


HT_COLS = T + 4

ENGS = ("pe", "act", "dve", "pool", "sp")
N_DMA_SEMS = 24


class Op:
    __slots__ = ("idx", "eng", "fn", "is_dma", "deps", "marked", "mark_val", "dma_sem", "dma_val")

    def __init__(self, idx, eng, fn, is_dma):
        self.idx = idx
        self.eng = eng
        self.fn = fn
        self.is_dma = is_dma
        self.deps = []
        self.marked = False
        self.mark_val = 0
        self.dma_sem = None
        self.dma_val = 0


class Prog:
    def __init__(self, nc):
        self.nc = nc
        self.ops = []
        self.bstate = {}
        self.es = ExitStack()
        self.cur_es = self.es
        self.ntile = 0
        self.bar_start = 0

    def sb(self, shape, dt=F32, name=None):
        self.ntile += 1
        return self.cur_es.enter_context(self.nc.sbuf_tensor(name or f"sb{self.ntile}", list(shape), dt))

    def ps(self, shape, dt=F32, name=None):
        self.ntile += 1
        return self.es.enter_context(self.nc.psum_tensor(name or f"ps{self.ntile}", list(shape), dt))

    def barrier(self):
        ops = self.ops
        last = {}
        dmas = []
        for o in ops[self.bar_start:]:
            if o.is_dma:
                dmas.append(o.idx)
            elif o.fn is not None:
                last[o.eng] = o.idx
        for e in ENGS:
            o = Op(len(ops), e, None, False)
            o.deps = [i for (en, i) in last.items() if en != e] + list(dmas)
            ops.append(o)
        self.bstate = {}
        self.bar_start = len(ops)

    class _Phase:
        def __init__(self, p):
            self.p = p

        def __enter__(self):
            self.es = ExitStack()
            self.prev = self.p.cur_es
            self.p.cur_es = self.es
            return self

        def __exit__(self, *a):
            self.p.barrier()
            self.p.cur_es = self.prev
            self.es.close()
            return False

    def phase(self):
        return Prog._Phase(self)

    def _st(self, b):
        k = b if isinstance(b, str) else id(b)
        st = self.bstate.get(k)
        if st is None:
            st = [None, {}, []]
            self.bstate[k] = st
        return st

    def op(self, eng, fn, reads=(), writes=(), dma=False):
        o = Op(len(self.ops), eng, fn, dma)
        hard = set()
        war = set()
        for b in reads:
            st = self._st(b)
            if st[0] is not None:
                hard.add(st[0])
        for b in writes:
            st = self._st(b)
            if st[0] is not None:
                hard.add(st[0])
            for r in st[1].values():
                war.add(r)
            for r in st[2]:
                war.add(r)
        for b in reads:
            st = self._st(b)
            if dma:
                st[2].append(o.idx)
            else:
                st[1][eng] = o.idx
        for b in writes:
            st = self._st(b)
            st[0] = o.idx
            st[1] = {}
            st[2] = []
        deps = []
        for d in hard | war:
            if d == o.idx:
                continue
            od = self.ops[d]
            if (not od.is_dma) and (not dma) and od.eng == eng:
                if eng == "pe":
                    continue
                if d not in hard:
                    continue
            deps.append(d)
        o.deps = deps
        self.ops.append(o)
        return o

    def dma(self, out, in_, reads=(), writes=(), q="sp", **kw):
        return self.op(q, lambda e: e.dma_start(out=out, in_=in_, **kw), reads, writes, dma=True)

    def mm(self, out, lhsT, rhs, start=True, stop=True, reads=(), writes=(), **kw):
        return self.op("pe", lambda e: e.matmul(out, lhsT, rhs, start=start, stop=stop, **kw), reads, writes)

    def tr(self, out, in_, ident, reads=(), writes=()):
        return self.op("pe", lambda e: e.transpose(out, in_, ident), reads, writes)

    def act(self, out, in_, func, reads=(), writes=(), **kw):
        return self.op("act", lambda e: e.activation(out, in_, func, **kw), reads, writes)

    def tt(self, out, in0, in1, op, reads=(), writes=(), eng="dve"):
        return self.op(eng, lambda e: e.tensor_tensor(out, in0, in1, op), reads, writes)

    def ts(self, out, in0, s1, s2, op0, op1=None, reads=(), writes=(), eng="dve"):
        if op1 is None:
            return self.op(eng, lambda e: e.tensor_scalar(out, in0, s1, None, op0), reads, writes)
        return self.op(eng, lambda e: e.tensor_scalar(out, in0, s1, s2, op0, op1), reads, writes)

    def stt(self, out, in0, scalar, in1, op0, op1, reads=(), writes=()):
        return self.op("dve", lambda e: e.scalar_tensor_tensor(out, in0, scalar, in1, op0, op1), reads, writes)

    def copy(self, out, in_, reads=(), writes=(), eng="dve"):
        if eng == "act":
            return self.op("act", lambda e: e.copy(out, in_), reads, writes)
        return self.op(eng, lambda e: e.tensor_copy(out, in_), reads, writes)

    def recip(self, out, in_, reads=(), writes=()):
        return self.op("dve", lambda e: e.reciprocal(out, in_), reads, writes)

    def memset(self, ap, val, writes=(), eng="dve"):
        return self.op(eng, lambda e: e.memset(ap, val), (), writes)

    def emit(self):
        nc = self.nc
        ops = self.ops
        for o in ops:
            for d in o.deps:
                ops[d].marked = True
        cnt = {e: 0 for e in ENGS}
        dma_cnt = {e: 0 for e in ENGS}
        dma_uses = {}
        for o in ops:
            if o.is_dma:
                k = dma_cnt[o.eng]
                dma_cnt[o.eng] += 1
                s = (o.eng, k % N_DMA_SEMS)
                dma_uses[s] = dma_uses.get(s, 0) + 1
                o.dma_sem = s
                o.dma_val = 16 * dma_uses[s]
            elif o.marked:
                cnt[o.eng] += 1
                o.mark_val = cnt[o.eng]
        es = self.es
        esem = {e: es.enter_context(nc.semaphore(f"s_{e}")) for e in ENGS if e != "sp"}
        dsem = {}
        for q in ENGS:
            for i in range(min(N_DMA_SEMS, dma_cnt[q])):
                dsem[(q, i)] = es.enter_context(nc.semaphore(f"d_{q}{i}"))
        block = es.enter_context(nc.Block())
        by_eng = {e: [o for o in ops if o.eng == e] for e in ENGS}
        final_dma = dict((s, 16 * n) for s, n in dma_uses.items())

        def run(ename, eobj):
            known = {}

            def wait(key, sem, val):
                if known.get(key, 0) >= val:
                    return
                eobj.wait_ge(sem, val)
                known[key] = val

            for o in by_eng[ename]:
                for d in o.deps:
                    od = ops[d]
                    if od.is_dma:
                        wait(od.dma_sem, dsem[od.dma_sem], od.dma_val)
                    else:
                        wait(od.eng, esem[od.eng], od.mark_val)
                if o.fn is None:
                    continue
                if o.is_dma:
                    if o.dma_val > 16:
                        wait(o.dma_sem, dsem[o.dma_sem], o.dma_val - 16)
                    o.fn(eobj).then_inc(dsem[o.dma_sem], 16)
                else:
                    ins = o.fn(eobj)
                    if o.marked:
                        ins.then_inc(esem[ename], 1)
            for s, v in final_dma.items():
                if s[0] == ename:
                    wait(s, dsem[s], v)

        if by_eng["sp"]:
            @block.sync
            def _(e):
                run("sp", e)
        if by_eng["pe"]:
            @block.tensor
            def _(e):
                run("pe", e)
        if by_eng["act"]:
            @block.scalar
            def _(e):
                run("act", e)
        if by_eng["dve"]:
            @block.vector
            def _(e):
                run("dve", e)
        if by_eng["pool"]:
            @block.gpsimd
            def _(e):
                run("pool", e)

    def close(self):
        self.es.close()


def featT(v):
    v = np.asarray(v)
    n = v.shape[-1] // 128
    return np.ascontiguousarray(np.moveaxis(v.reshape(v.shape[:-1] + (n, 128)), -1, -2))


def rope_tables(rot_dim):
    t = np.arange(T_LAT)
    rows = (t // GRID_W).astype(np.float64)
    cols = (t % GRID_W).astype(np.float64)
    per_axis = rot_dim // 2
    inv = 10000.0 ** (-np.arange(0, per_axis, 2, dtype=np.float64) / per_axis)
    ang = np.concatenate([rows[:, None] * inv, cols[:, None] * inv], axis=-1)
    cos = np.ones((rot_dim, T), np.float32)
    sin = np.zeros((rot_dim, T), np.float32)
    cos[:, T_CTX:] = np.repeat(np.cos(ang).T, 2, axis=0)
    sin[:, T_CTX:] = np.repeat(np.sin(ang).T, 2, axis=0)
    return cos, sin


def rot_lhsT(n, off=0, size=None):
    size = size or n
    m = np.zeros((size, size), np.float32)
    for i in range(n // 2):
        m[off + 2 * i + 1, off + 2 * i] = -1.0
        m[off + 2 * i, off + 2 * i + 1] = 1.0
    return m


def na_plan():
    rows = T_LAT // GRID_W
    wr, wc = 8, 16
    r_idx = np.arange(rows)
    key_lo = np.clip(r_idx - wr // 2, 0, rows - wr)
    c_idx = np.arange(GRID_W)
    col_start = np.clip(c_idx - wc // 2, 0, GRID_W - wc)
    col_ok = (c_idx[None, :] >= col_start[:, None]) & (c_idx[None, :] < col_start[:, None] + wc)
    masks, mkey, plan = [], {}, []
    for qb in range(8):
        r0 = qb * 8
        lo = max(r0 - 4, 0)
        hi = min(r0 + 12, rows)
        ent = []
        for kr0 in range(lo, hi, 2):
            m = np.full((2, 64, 8, 64), NEG, np.float32)
            for kl in range(2):
                for rl in range(8):
                    r = r0 + rl
                    kr = kr0 + kl
                    if key_lo[r] <= kr < key_lo[r] + wr:
                        m[kl, :, rl, :] = np.where(col_ok.T, 0.0, NEG)
            if not (m == 0).any():
                continue
            key = m.tobytes()
            if key not in mkey:
                mkey[key] = len(masks)
                masks.append(m.reshape(128, 512))
            ent.append((2 + kr0 // 2, kr0 - r0, mkey[key]))
        plan.append(ent)
    return plan, np.stack(masks, 0)


NA_PLAN, NA_MASKS = na_plan()
NM = NA_MASKS.shape[0]


def na_bias_table(rpb):
    L = rpb.shape[0]
    ck = np.arange(64)[:, None]
    cq = np.arange(64)[None, :]
    dc = np.clip(ck - cq + 15, 0, 30)
    out = np.zeros((L, 4, 2, 64, 22, 64), np.float32)
    for kl in range(2):
        for j in range(22):
            dr = 17 - j + kl
            if 0 <= dr <= 14:
                out[:, :, kl, :, j, :] = rpb[:, :, dr][:, :, dc]
    return out.reshape(L, 4, 128, 22, 64)


RW_LOCK = 8


def build(nl=DEPTH, dbg=(), mixers=("na", "gqa", "mla", "rw"), do_mlp=True, force_ctx=None):
    nc = bass.Bass("TRN2", target_bir_lowering=False)
    P = Prog(nc)

    def din(name, shape, dt=F32):
        return nc.dram_tensor(name, list(shape), dt, kind="ExternalInput").ap()

    def dscr(name, shape, dt=F32):
        return nc.dram_tensor(name, list(shape), dt, kind="Internal").ap()

    def dout(name, shape, dt=F32):
        return nc.dram_tensor(name, list(shape), dt, kind="ExternalOutput").ap()

    x_in = din("x", [T_LAT, D])
    ctx_in = din("ctx", [T_CTX, D])
    cvec = din("cvec", [128, 8, 2])
    w_mod = din("w_mod", [DEPTH, D, 6 * D])
    b_modT = din("b_modT", [DEPTH, 128, 48])
    n1g = din("n1g", [DEPTH, 128, 8])
    n2g = din("n2g", [DEPTH, 128, 8])
    fng = din("fng", [128, 8])
    w_in = din("w_in", [DEPTH, D, IN_COLS])
    w_kr96 = din("w_kr96", [DEPTH, D, 96])
    gqa_qn = din("gqa_qn", [DEPTH, 128, 1])
    gqa_kn = din("gqa_kn", [DEPTH, 128, 1])
    mla_qn = din("mla_qn", [DEPTH, 128, 2])
    mla_kvn = din("mla_kvn", [DEPTH, 128, 1])
    w_uq = din("w_uq", [DEPTH, 256, 384])
    w_ukv = din("w_ukv", [DEPTH, 128, 512])
    na_tb = din("na_tb", [DEPTH, 4, 128, 22, 64])
    na_rpbf = din("na_rpbf", [DEPTH, 4, 1, 465])
    na_masks = din("na_masks", [NM, 128, 512], BF16)
    w_out = din("w_out", [DEPTH, D, D])
    w_fc1 = din("w_fc1", [DEPTH, D, 4 * D])
    w_fc2 = din("w_fc2", [DEPTH, 4 * D, D])
    c_ident = din("c_ident", [128, 128])
    c_bd64 = din("c_bd64", [128, 128])
    c_rt128 = din("c_rt128", [128, 128])
    c_rt96 = din("c_rt96", [96, 96])
    c_cos128 = din("c_cos128", [128, T])
    c_sin128 = din("c_sin128", [128, T])
    c_cos96 = din("c_cos96", [96, T])
    c_sin96 = din("c_sin96", [96, T])
    rw_tap = din("rw_tap", [DEPTH, 128, 10, 3])
    rw_vec = din("rw_vec", [DEPTH, 128, 5, 2, 2])
    rw_w2 = din("rw_w2", [DEPTH, 128, 256])
    rw_a2 = din("rw_a2", [DEPTH, 128, 256])
    rw_g2 = din("rw_g2", [DEPTH, 160, 256])
    rw_ln = din("rw_ln", [DEPTH, 128, 2, 2])
    c_tri = din("c_tri", [4, 128, 128])
    c_blk = din("c_blk", [4, 128, 128])
    y_out = dout("y", [T_LAT, D])

    xT = dscr("xT", [8, 128, T])
    h2T = dscr("h2T", [8, 128, T], BF16)
    mixT = dscr("mixT", [8, 128, T], BF16)
    na_QT = dscr("na_QT", [4, 64, T], BF16); na_KT = dscr("na_KT", [4, 64, T], BF16); na_V = dscr("na_V", [T, 256], BF16)
    gq_QT = dscr("gq_QT", [4, 64, T], BF16); gq_KT = dscr("gq_KT", [2, 64, T], BF16); gq_V = dscr("gq_V", [T, 128], BF16)
    ml_QT = dscr("ml_QT", [4, 96, T], BF16); ml_KT = dscr("ml_KT", [4, 96, T], BF16); ml_V = dscr("ml_V", [T, 256], BF16)

    rw_fm = dscr("rw_fm", [2, 4, 2, 128, T])
    rw_tm = dscr("rw_tm", [2, 2, T, 256])
    rw_v = dscr("rw_v", [T, 256])
    rw_vT = dscr("rw_vT", [2, 128, T])
    rw_gate = dscr("rw_gate", [2, 128, T])
    rw_zz = dscr("rw_zz", [2, 128, T])

    pcs = P.sb([128, 2, 2, NKT])
    ident = P.sb([128, 128]); ones_f = P.sb([128, 128]); bd64 = P.sb([128, 128])
    ones_bf = P.sb([128, 128], BF16)
    rt128 = P.sb([128, 128]); rt96 = P.sb([96, 96])
    cs = P.sb([128, 8, 2])
    modT = P.sb([128, 48, 2])
    gs = P.sb([128, 8, 2]); sh = P.sb([128, 8, 2])
    bm = P.sb([128, 48]); ng1 = P.sb([128, 8]); ng2 = P.sb([128, 8]); fg = P.sb([128, 8])
    eps_t = P.sb([128, 1])
    stat = P.sb([128, 3, 2, 4, 9])
    negm = P.sb([128, 12])
    smallv = P.sb([128, 8])
    pb = [P.ps([128, 512]) for _ in range(8)]
    cnt = {}

    def nxt(name, arr):
        i = cnt.get(name, 0)
        cnt[name] = i + 1
        return arr[i % len(arr)]

    def npb():
        return nxt("pb", pb)

    P.dma(ident[:], c_ident, writes=[ident])
    P.dma(bd64[:], c_bd64, writes=[bd64])
    P.dma(rt128[:], c_rt128, writes=[rt128])
    P.dma(rt96[:], c_rt96, writes=[rt96])
    P.dma(fg[:], fng, writes=[fg])
    P.memset(ones_f[:], 1.0, writes=[ones_f])
    P.memset(ones_bf[:], 1.0, writes=[ones_bf])
    P.memset(eps_t[:], RMS_EPS, writes=[eps_t])
    P.dma(cs[:], cvec, writes=[cs])
    P.act(cs[:], cs[:], AF.Silu, reads=[cs], writes=[cs])

    def dump(name, ap, shape, reads, dt=F32):
        if name in dbg or name.split("_")[0][:2] == "rw" and any(x.startswith("rw") for x in dbg):
            d = dout("dbg_" + name, shape, dt)
            P.dma(d, ap, reads=reads, q="pool")

    def norm_block(xb, n, out_fn, scale_fn, bias_fn, T_):
        sq = nxt("big", T_["big"])
        P.act(sq[:, :, 0:n], xb[:, :, 0:n], AF.Square, reads=[xb], writes=[sq])
        ps = npb()
        for ft in range(8):
            P.mm(ps[:, 0:n], ones_f[:], sq[:, ft, 0:n], start=(ft == 0), stop=(ft == 7), reads=[ones_f, sq], writes=[ps])
        rs = nxt("rs", T_["rs"])
        P.act(rs[:, 0:n], ps[:, 0:n], AF.Sqrt, reads=[ps, eps_t], writes=[rs], scale=1.0 / D, bias=eps_t[:, 0:1])
        P.recip(rs[:, 0:n], rs[:, 0:n], reads=[rs], writes=[rs])
        for ft in range(8):
            tmp = nxt("tmp", T_["tmp"])
            P.tt(tmp[:, 0:n], xb[:, ft, 0:n], rs[:, 0:n], ALU.mult, reads=[xb, rs], writes=[tmp])
            dst, wk = out_fn(ft)
            b = bias_fn(ft)
            if b is None:
                P.act(dst, tmp[:, 0:n], AF.Identity, reads=[tmp, gs, sh, fg], writes=wk, scale=scale_fn(ft))
            else:
                P.act(dst, tmp[:, 0:n], AF.Identity, reads=[tmp, gs, sh, fg], writes=wk, scale=scale_fn(ft), bias=b)

    def set_norm_params(which):
        base = 0 if which == 1 else 24
        ng = ng1 if which == 1 else ng2
        for s in range(2):
            P.ts(gs[:, :, s], modT[:, base + 8:base + 16, s], 1.0, None, ALU.add, reads=[modT], writes=[gs])
            P.tt(gs[:, :, s], gs[:, :, s], ng[:], ALU.mult, reads=[gs, ng], writes=[gs])
            P.copy(sh[:, :, s], modT[:, base:base + 8, s], reads=[modT], writes=[sh])

    def load_w_bf16(dst, dkey, src, K, ncols, T_, engs=("dve", "pool")):
        kt_n = K // 128
        per = 4096 // kt_n
        c = 0
        i = 0
        while c < ncols:
            w = min(per, ncols - c)
            st = nxt("big", T_["big"])
            sv = st[:].rearrange("p a t -> p (a t)")[:, 0:kt_n * w].rearrange("p (k n) -> p k n", k=kt_n)
            P.dma(sv, src[:, c:c + w].rearrange("(kt p) n -> p kt n", p=128), writes=[st])
            P.copy(dst[:, :, c:c + w], sv, reads=[st], writes=[dkey], eng=engs[i % len(engs)])
            c += w
            i += 1

    def stage_A():
        with P.phase():
            big = [P.sb([128, 8, 512]) for _ in range(4)]
            for bi, (g0, n) in enumerate(BLOCKS):
                na = n // 128
                xin = nxt("bigA", big)
                src = ctx_in if g0 < T_CTX else x_in[g0 - T_CTX:g0 - T_CTX + n, :]
                xv = xin[:].rearrange("p a t -> p (a t)")[:, 0:na * 1024].rearrange("p (a f) -> p a f", a=na)
                P.dma(xv, src.rearrange("(a p) f -> p a f", p=128), writes=[xin])
                xb = nxt("bigA", big)
                for ft in range(8):
                    ps = npb()
                    for a in range(na):
                        P.tr(ps[:, a * 128:(a + 1) * 128], xv[:, a, ft * 128:(ft + 1) * 128], ident[:],
                             reads=[xin, ident], writes=[ps])
                    P.copy(xb[:, ft, 0:n], ps[:, 0:n], reads=[ps], writes=[xb], eng=("act" if ft % 2 else "dve"))
                P.dma(xT[:, :, g0:g0 + n].rearrange("a p t -> p a t"), xb[:, :, 0:n], reads=[xb], writes=[f"xT{bi}"], q="pool")

    def stage_M(L, T_):
        P.dma(bm[:], b_modT[L], writes=[bm])
        P.dma(ng1[:], n1g[L], writes=[ng1])
        P.dma(ng2[:], n2g[L], writes=[ng2])
        for jc in range(12):
            wm = nxt("big", T_["big"])
            P.dma(wm[:], w_mod[L][:, jc * 512:(jc + 1) * 512].rearrange("(kt p) n -> p kt n", p=128), writes=[wm])
            ps = npb()
            for j4 in range(4):
                for kt in range(8):
                    P.mm(ps[:, j4 * 2:j4 * 2 + 2], wm[:, kt, j4 * 128:(j4 + 1) * 128], cs[:, kt, :],
                         start=(kt == 0), stop=(kt == 7), reads=[wm, cs], writes=[ps])
            for j4 in range(4):
                j = jc * 4 + j4
                P.ts(modT[:, j, :], ps[:, j4 * 2:j4 * 2 + 2], bm[:, j:j + 1], None, ALU.add,
                     reads=[ps, bm], writes=[modT])

    def phase1(L):
        with P.phase():
            hT = P.sb([128, 8, HT_COLS], BF16)
            P.memset(hT[:].rearrange("p a t -> p (a t)"), 0.0, writes=[hT], eng="pool")
            phase1a(L, hT)
            if "rw" in mixers:
                rw_prep(L, hT)

    def phase1a(L, hT):
        with P.phase():
            T_ = {"big": [P.sb([128, 8, 512]) for _ in range(2)],
                  "tmp": [P.sb([128, 512]) for _ in range(6)],
                  "rs": [P.sb([128, 512]) for _ in range(2)]}
            wb = P.sb([128, 8, 768], BF16)
            ob = [P.sb([128, 512], BF16) for _ in range(4)]
            cst = [P.sb([128, 512]) for _ in range(4)]
            stage_M(L, T_)
            set_norm_params(1)
            P.dma(smallv[:, 0:1], gqa_qn[L], writes=[smallv])
            P.dma(smallv[:, 1:2], gqa_kn[L], writes=[smallv])
            P.dma(smallv[:, 2:4], mla_qn[L], writes=[smallv])
            P.dma(smallv[:, 4:5], mla_kvn[L], writes=[smallv])
            for bi, (g0, n) in enumerate(BLOCKS):
                xb = nxt("big", T_["big"])
                P.dma(xb[:, :, 0:n], xT[:, :, g0:g0 + n].rearrange("a p t -> p a t"), reads=[f"xT{bi}"], writes=[xb])
                s = 1 if g0 < T_CTX else 0
                c0 = hc(g0)
                norm_block(xb, n, (lambda ft, c0=c0, n=n: (hT[:, ft, c0:c0 + n], [hT])),
                           (lambda ft, s=s: gs[:, ft, s:s + 1]), (lambda ft, s=s: sh[:, ft, s:s + 1]), T_)
            dump("hT", hT[:], [128, 8, HT_COLS], [hT], BF16)
            dump("modT", modT[:], [128, 48, 2], [modT])

            def proj_fm(c0, m, g0, n, ps, wt=wb, wkey="wb"):
                c = hc(g0)
                for kt in range(8):
                    P.mm(ps[0:m, 0:n], wt[:, kt, c0:c0 + m], hT[:, kt, c:c + n], start=(kt == 0), stop=(kt == 7),
                         reads=[wkey, hT], writes=[ps])

            def proj_tm(c0, ncols, g, ps, col_off=0):
                c = hc(g)
                for kt in range(8):
                    P.mm(ps[:, col_off:col_off + ncols], hT[:, kt, c:c + 128], wb[:, kt, c0:c0 + ncols],
                         start=(kt == 0), stop=(kt == 7), reads=["wb", hT], writes=[ps])

            def norm2_stat(src_ap, rows, n, mixer, qk, head, bi, skey, base=0):
                sq = nxt("tmp", T_["tmp"])
                P.act(sq[base:base + rows, 0:n], src_ap, AF.Square, reads=[skey], writes=[sq])
                ps = npb()
                P.mm(ps[:, 0:n], ones_f[base:base + rows, :], sq[base:base + rows, 0:n], reads=[ones_f, sq], writes=[ps])
                P.op("dve", lambda e: e.reduce_max(stat[:, mixer, qk, head, bi:bi + 1], ps[:, 0:n], AX.X),
                     reads=[ps], writes=[stat])

            def store_v(ps, ncols, g, dst):
                o = nxt("ob", ob)
                P.copy(o[:, 0:ncols], ps[:, 0:ncols], reads=[ps], writes=[o], eng="act")
                P.dma(dst[g:g + 128, :], o[:, 0:ncols], reads=[o], writes=[], q="pool")

            if "na" in mixers:
                load_w_bf16(wb[:, :, 0:768], "wb", w_in[L][:, NA_C0:NA_C0 + 768], D, 768, T_)
                for bi, (g0, n) in enumerate(BLOCKS):
                    for qk, dst in ((0, na_QT), (1, na_KT)):
                        for j in range(2):
                            ps = npb()
                            proj_fm(qk * 256 + j * 128, 128, g0, n, ps)
                            o = nxt("ob", ob)
                            P.copy(o[:, 0:n], ps[:, 0:n], reads=[ps], writes=[o], eng="act")
                            P.dma(dst[2 * j:2 * j + 2, :, g0:g0 + n].rearrange("h d t -> (h d) t"), o[:, 0:n], reads=[o], q="pool")
                            for hh in range(2):
                                norm2_stat(o[hh * 64:(hh + 1) * 64, 0:n], 64, n, 0, qk, 2 * j + hh, bi, o, base=hh * 64)
                    for a in range(n // 128):
                        ps = npb()
                        proj_tm(512, 256, g0 + a * 128, ps)
                        store_v(ps, 256, g0 + a * 128, na_V)

            if "gqa" in mixers:
                load_w_bf16(wb[:, :, 0:512], "wb", w_in[L][:, GQA_C0:GQA_C0 + 512], D, 512, T_)
                for bi, (g0, n) in enumerate(BLOCKS):
                    ct_ = nxt("cst", cst); st_ = nxt("cst", cst)
                    P.dma(ct_[:, 0:n], c_cos128[:, g0:g0 + n], writes=[ct_])
                    P.dma(st_[:, 0:n], c_sin128[:, g0:g0 + n], writes=[st_])
                    for (c0, dst, h0, gcol, qk) in ((0, gq_QT, 0, 0, 0), (128, gq_QT, 2, 0, 0), (256, gq_KT, 0, 1, 1)):
                        ps = npb()
                        proj_fm(c0, 128, g0, n, ps)
                        sq = nxt("tmp", T_["tmp"])
                        P.act(sq[:, 0:n], ps[:, 0:n], AF.Square, reads=[ps], writes=[sq])
                        ps2 = npb()
                        P.mm(ps2[:, 0:n], bd64[:], sq[:, 0:n], reads=[bd64, sq], writes=[ps2])
                        rs = nxt("rs", T_["rs"])
                        P.act(rs[:, 0:n], ps2[:, 0:n], AF.Sqrt, reads=[ps2, eps_t], writes=[rs], scale=1.0 / 64, bias=eps_t[:, 0:1])
                        P.recip(rs[:, 0:n], rs[:, 0:n], reads=[rs], writes=[rs])
                        qn = nxt("tmp", T_["tmp"])
                        P.stt(qn[:, 0:n], ps[:, 0:n], smallv[:, gcol:gcol + 1], rs[:, 0:n], ALU.mult, ALU.mult,
                              reads=[ps, smallv, rs], writes=[qn])
                        ps3 = npb()
                        P.mm(ps3[:, 0:n], rt128[:], qn[:, 0:n], reads=[rt128, qn], writes=[ps3])
                        t1 = nxt("tmp", T_["tmp"])
                        P.tt(t1[:, 0:n], qn[:, 0:n], ct_[:, 0:n], ALU.mult, reads=[qn, ct_], writes=[t1])
                        t2 = nxt("tmp", T_["tmp"])
                        P.tt(t2[:, 0:n], ps3[:, 0:n], st_[:, 0:n], ALU.mult, reads=[ps3, st_], writes=[t2])
                        o = nxt("ob", ob)
                        P.tt(o[:, 0:n], t1[:, 0:n], t2[:, 0:n], ALU.add, reads=[t1, t2], writes=[o])
                        P.dma(dst[h0:h0 + 2, :, g0:g0 + n].rearrange("h d t -> (h d) t"), o[:, 0:n], reads=[o], q="pool")
                        for hh in range(2):
                            norm2_stat(o[hh * 64:(hh + 1) * 64, 0:n], 64, n, 1, qk, h0 + hh, bi, o, base=hh * 64)
                    for a in range(n // 128):
                        ps = npb()
                        proj_tm(384, 128, g0 + a * 128, ps)
                        store_v(ps, 128, g0 + a * 128, gq_V)

            if "mla" in mixers:
                load_w_bf16(wb[:, :, 0:384], "wb", w_in[L][:, MLA_C0:MLA_C0 + 384], D, 384, T_)
                load_w_bf16(wb[:, :, 384:480], "wb", w_kr96[L], D, 96, T_)
                wuq = P.sb([128, 2, 384], BF16)
                wukv = P.sb([128, 1, 512], BF16)
                load_w_bf16(wuq[:], wuq, w_uq[L], 256, 384, T_)
                load_w_bf16(wukv[:], wukv, w_ukv[L], 128, 512, T_)
                qlnb = [P.sb([128, 2, 512], BF16) for _ in range(2)]
                ckb = [P.sb([128, 512], BF16) for _ in range(2)]
                for bi, (g0, n) in enumerate(BLOCKS):
                    ct_ = nxt("cst", cst); st_ = nxt("cst", cst)
                    P.dma(ct_[0:96, 0:n], c_cos96[:, g0:g0 + n], writes=[ct_])
                    P.dma(st_[0:96, 0:n], c_sin96[:, g0:g0 + n], writes=[st_])

                    def rope96(src_ps, skey):
                        qf = nxt("tmp", T_["tmp"])
                        P.copy(qf[0:96, 0:n], src_ps[0:96, 0:n], reads=[skey], writes=[qf], eng="act")
                        ps3 = npb()
                        P.mm(ps3[0:96, 0:n], rt96[:], qf[0:96, 0:n], reads=[rt96, qf], writes=[ps3])
                        t1 = nxt("tmp", T_["tmp"])
                        P.tt(t1[0:96, 0:n], qf[0:96, 0:n], ct_[0:96, 0:n], ALU.mult, reads=[qf, ct_], writes=[t1])
                        t2 = nxt("tmp", T_["tmp"])
                        P.tt(t2[0:96, 0:n], ps3[0:96, 0:n], st_[0:96, 0:n], ALU.mult, reads=[ps3, st_], writes=[t2])
                        P.tt(t1[0:96, 0:n], t1[0:96, 0:n], t2[0:96, 0:n], ALU.add, reads=[t1, t2], writes=[t1])
                        return t1

                    psq = [npb(), npb()]
                    sqs = []
                    for ft in range(2):
                        proj_fm(ft * 128, 128, g0, n, psq[ft])
                        sq = nxt("tmp", T_["tmp"])
                        P.act(sq[:, 0:n], psq[ft][:, 0:n], AF.Square, reads=[psq[ft]], writes=[sq])
                        sqs.append(sq)
                    ps2 = npb()
                    for ft in range(2):
                        P.mm(ps2[:, 0:n], ones_f[:], sqs[ft][:, 0:n], start=(ft == 0), stop=(ft == 1), reads=[ones_f, sqs[ft]], writes=[ps2])
                    rs = nxt("rs", T_["rs"])
                    P.act(rs[:, 0:n], ps2[:, 0:n], AF.Sqrt, reads=[ps2, eps_t], writes=[rs], scale=1.0 / 256, bias=eps_t[:, 0:1])
                    P.recip(rs[:, 0:n], rs[:, 0:n], reads=[rs], writes=[rs])
                    ql = nxt("qlnb", qlnb)
                    for ft in range(2):
                        P.stt(ql[:, ft, 0:n], psq[ft][:, 0:n], smallv[:, 2 + ft:3 + ft], rs[:, 0:n], ALU.mult, ALU.mult,
                              reads=[psq[ft], smallv, rs], writes=[ql])
                    for h in range(4):
                        ps = npb()
                        for ft in range(2):
                            P.mm(ps[0:96, 0:n], wuq[:, ft, h * 96:(h + 1) * 96], ql[:, ft, 0:n], start=(ft == 0), stop=(ft == 1),
                                 reads=[wuq, ql], writes=[ps])
                        y = rope96(ps, ps)
                        o = nxt("ob", ob)
                        P.copy(o[0:96, 0:n], y[0:96, 0:n], reads=[y], writes=[o])
                        P.dma(ml_QT[h, :, g0:g0 + n], o[0:96, 0:n], reads=[o], q="pool")
                        norm2_stat(y[0:96, 0:n], 96, n, 2, 0, h, bi, y)
                    psc = npb()
                    proj_fm(256, 128, g0, n, psc)
                    sq = nxt("tmp", T_["tmp"])
                    P.act(sq[:, 0:n], psc[:, 0:n], AF.Square, reads=[psc], writes=[sq])
                    ps2 = npb()
                    P.mm(ps2[:, 0:n], ones_f[:], sq[:, 0:n], reads=[ones_f, sq], writes=[ps2])
                    rs = nxt("rs", T_["rs"])
                    P.act(rs[:, 0:n], ps2[:, 0:n], AF.Sqrt, reads=[ps2, eps_t], writes=[rs], scale=1.0 / 128, bias=eps_t[:, 0:1])
                    P.recip(rs[:, 0:n], rs[:, 0:n], reads=[rs], writes=[rs])
                    ck = nxt("ckb", ckb)
                    P.stt(ck[:, 0:n], psc[:, 0:n], smallv[:, 4:5], rs[:, 0:n], ALU.mult, ALU.mult, reads=[psc, smallv, rs], writes=[ck])
                    psk = npb()
                    proj_fm(384, 96, g0, n, psk)
                    kr = rope96(psk, psk)
                    for h in range(4):
                        ps = npb()
                        P.mm(ps[0:64, 0:n], wukv[:, 0, h * 64:(h + 1) * 64], ck[:, 0:n], reads=[wukv, ck], writes=[ps])
                        kf = nxt("tmp", T_["tmp"])
                        P.copy(kf[0:96, 0:n], kr[0:96, 0:n], reads=[kr], writes=[kf])
                        P.copy(kf[0:64, 0:n], ps[0:64, 0:n], reads=[ps], writes=[kf], eng="act")
                        o = nxt("ob", ob)
                        P.copy(o[0:96, 0:n], kf[0:96, 0:n], reads=[kf], writes=[o])
                        P.dma(ml_KT[h, :, g0:g0 + n], o[0:96, 0:n], reads=[o], q="pool")
                        norm2_stat(kf[0:96, 0:n], 96, n, 2, 1, h, bi, kf)
                    for a in range(n // 128):
                        ps = npb()
                        P.mm(ps[:, 0:256], ck[:, a * 128:(a + 1) * 128], wukv[:, 0, 256:512], reads=[ck, wukv], writes=[ps])
                        store_v(ps, 256, g0 + a * 128, ml_V)

            red = P.sb([128, 3, 2, 4])
            P.op("dve", lambda e: e.tensor_reduce(red[:].rearrange("p a b c -> p (a b c)"),
                                                  stat[:].rearrange("p a b c d -> p (a b c) d"), AX.X, ALU.max),
                 reads=[stat], writes=[red])
            for mi, (mx, scale) in enumerate((("na", 64 ** -0.5), ("gqa", 64 ** -0.5), ("mla", 96 ** -0.5))):
                if mx not in mixers:
                    continue
                for h in range(4):
                    kh = h // 2 if mx == "gqa" else h
                    col = mi * 4 + h
                    P.tt(negm[:, col:col + 1], red[:, mi, 0, h:h + 1], red[:, mi, 1, kh:kh + 1], ALU.mult, reads=[red], writes=[negm])
                    P.act(negm[:, col:col + 1], negm[:, col:col + 1], AF.Sqrt, reads=[negm], writes=[negm], scale=scale * scale)
                    if mx == "na":
                        rp = nxt("tmp", T_["tmp"])
                        P.dma(rp[:, 0:465], na_rpbf[L, h].partition_broadcast(128), writes=[rp])
                        bmx = nxt("rs", T_["rs"])
                        P.op("dve", lambda e, bmx=bmx, rp=rp: e.tensor_reduce(bmx[:, 0:1], rp[:, 0:465], AX.X, ALU.max, apply_absolute_value=True),
                             reads=[rp], writes=[bmx])
                        P.tt(negm[:, col:col + 1], negm[:, col:col + 1], bmx[:, 0:1], ALU.add, reads=[negm, bmx], writes=[negm])
                    P.ts(negm[:, col:col + 1], negm[:, col:col + 1], -1.0, None, ALU.mult, reads=[negm], writes=[negm])
            dump("negm", negm[:], [128, 12], [negm])

    def phase2(L, want_ctx):
        with P.phase():
            QTt = [P.sb([128, T], BF16) for _ in range(2)]
            KTt = [P.sb([128, T], BF16) for _ in range(2)]
            for t__ in QTt + KTt:
                P.memset(t__[:], 0.0, writes=[t__], eng="pool")
            Vt = P.sb([128, NKT, 256], BF16)
            Va = P.sb([128, NKT, 4, 128], BF16)
            pT = [P.sb([128, 512], BF16) for _ in range(3)]
            rl = [P.sb([64, 512]) for _ in range(2)]
            ob = [P.sb([64, 512], BF16) for _ in range(2)]
            tb = [P.sb([128, 22, 64]) for _ in range(2)]
            mk = [P.sb([128, 512], BF16) for _ in range(3)]
            bt = [P.sb([128, 512]) for _ in range(3)]
            sps = pb[0:4]
            ops_ = pb[4:7]
            P.memset(Va[:].rearrange("p a b c -> p (a b c)"), 1.0, writes=[Va], eng="pool")
            specs = []
            if "na" in mixers:
                specs.append(("na", 0, na_QT, na_KT, na_V, 4, 64, 64 ** -0.5))
            if "gqa" in mixers:
                specs.append(("gqa", 1, gq_QT, gq_KT, gq_V, 2, 64, 64 ** -0.5))
            if "mla" in mixers:
                specs.append(("mla", 2, ml_QT, ml_KT, ml_V, 4, 96, 96 ** -0.5))
            for (mx, mi, dQ, dK, dV, nkv, dk, scale) in specs:
                P.dma(Vt[:, :, 0:nkv * 64], dV.rearrange("(kt p) c -> p kt c", p=128), writes=[Vt])
                for kh_ in range(nkv):
                    P.copy(Va[:, :, kh_, 0:64], Vt[:, :, kh_ * 64:(kh_ + 1) * 64], reads=[Vt], writes=[Va], eng=("pool" if kh_ % 2 else "dve"))
                for h in range(4):
                    QT = nxt("QT", QTt); KT = nxt("KT", KTt)
                    kh = h // 2 if mx == "gqa" else h
                    P.dma(QT[0:dk, :], dQ[h], writes=[QT])
                    P.dma(KT[0:dk, :], dK[kh], writes=[KT])
                    col = mi * 4 + h
                    if mx == "na":
                        tbh = nxt("tb", tb)
                        P.dma(tbh[:], na_tb[L, h], writes=[tbh])
                    for bi, (g0, n) in enumerate(BLOCKS):
                        if g0 < T_CTX:
                            if not want_ctx:
                                continue
                            kts = [(0, None), (1, None)]
                        elif mx == "na":
                            kts = [(0, None), (1, None)] + [(kt, (Dd, mi_)) for (kt, Dd, mi_) in NA_PLAN[bi - 1]]
                        else:
                            kts = [(kt, None) for kt in range(NKT)]
                        o_ps = nxt("ops", ops_)
                        nk = len(kts)

                        def issue_S(i):
                            sp = nxt("sps", sps)
                            kt = kts[i][0]
                            P.mm(sp[:, 0:n], KT[:, kt * 128:(kt + 1) * 128], QT[:, g0:g0 + n], reads=[KT, QT], writes=[sp])
                            return sp
                        S = [issue_S(i) for i in range(min(3, nk))]
                        for i, (kt, bias) in enumerate(kts):
                            sp = S[i]
                            p = nxt("pT", pT)
                            if bias is None:
                                P.act(p[:, 0:n], sp[:, 0:n], AF.Exp, reads=[sp, negm], writes=[p], scale=scale, bias=negm[:, col:col + 1])
                            else:
                                Dd, mi_ = bias
                                m_ = nxt("mk", mk)
                                P.dma(m_[:], na_masks[mi_], writes=[m_])
                                b_ = nxt("bt", bt)
                                j0 = 10 - Dd
                                P.tt(b_[:], tbh[:, j0:j0 + 8, :].rearrange("p a b -> p (a b)"), m_[:], ALU.add, reads=[tbh, m_], writes=[b_], eng="pool")
                                P.stt(b_[:], sp[:, 0:n], scale, b_[:], ALU.mult, ALU.add, reads=[sp, b_], writes=[b_])
                                P.act(p[:, 0:n], b_[:, 0:n], AF.Exp, reads=[b_, negm], writes=[p], scale=1.0, bias=negm[:, col:col + 1])
                            if i + 3 < nk:
                                S.append(issue_S(i + 3))
                            P.mm(o_ps[:, 0:n], Va[:, kt, kh, :], p[:, 0:n], start=(i == 0), stop=(i == nk - 1),
                                 reads=[Va, p], writes=[o_ps])
                        r_ = nxt("rl", rl)
                        P.recip(r_[:, 0:n], o_ps[64:128, 0:n], reads=[o_ps], writes=[r_])
                        o = nxt("ob2", ob)
                        P.tt(o[:, 0:n], o_ps[0:64, 0:n], r_[:, 0:n], ALU.mult, reads=[o_ps, r_], writes=[o])
                        f = mi * 256 + h * 64
                        P.dma(mixT[f // 128, f % 128:f % 128 + 64, g0:g0 + n], o[:, 0:n], reads=[o], writes=[f"mix{f}_{bi}"], q="pool")

    RWB = [(0, 256)] + [(256 + 256 * i, 256) for i in range(16)]
    DECAY_C = -0.6065306597126334

    def rw_prep(L, hT):
        with P.phase():
            wr = P.sb([128, 8, 1184], BF16)
            with P.phase():
                big = {"big": [P.sb([128, 8, 512]) for _ in range(2)]}
                load_w_bf16(wr[:], wr, w_in[L][:, RW_C0:RW_C0 + 1184], D, 1184, big)
            tap = P.sb([128, 10, 3]); vec = P.sb([128, 5, 2, 2]); w2s = P.sb([128, 256]); a2s = P.sb([128, 256])
            g2a = P.sb([128, 256]); g2b = P.sb([32, 256])
            P.dma(tap[:], rw_tap[L], writes=[tap]); P.dma(vec[:], rw_vec[L], writes=[vec])
            P.dma(w2s[:], rw_w2[L], writes=[w2s]); P.dma(a2s[:], rw_a2[L], writes=[a2s])
            P.dma(g2a[:], rw_g2[L][0:128, :], writes=[g2a]); P.dma(g2b[:], rw_g2[L][128:160, :], writes=[g2b])
            U = [P.sb([128, 10, 256]) for _ in range(2)]
            Rs = [{k: P.sb([128, 256]) for k in ("th", "sg0", "sg1", "lw", "asg", "kkr", "sq", "nr", "kk", "b", "t", "kd",
                                                "cs", "cl", "clx", "e1", "e2", "e3", "bh", "kh", "tz")} for _ in range(2)]
            R = Rs[0]
            gate_t = [P.sb([128, 256]) for _ in range(2)]
            zz = P.sb([128, 2, 256])
            vt = P.sb([128, 2, 256])
            fmout = [P.sb([128, 4, 2, 256]) for _ in range(2)]
            tmout = [P.sb([128, 2, 2, 256]) for _ in range(2)]
            for bi, (g0, n) in enumerate(RWB):
                c = hc(g0)
                u = nxt("U", U)
                uk = lambda ti: f"u{id(u)}_{ti}"
                for ti in range(10):
                    m = 128 if ti < 9 else 32
                    ps = npb()
                    for kt in range(8):
                        P.mm(ps[0:m, 0:n + 2], wr[:, kt, ti * 128:ti * 128 + m], hT[:, kt, c - 1:c + n + 1],
                             start=(kt == 0), stop=(kt == 7), reads=[wr, hT], writes=[ps])
                    P.ts(u[0:m, ti, :], ps[0:m, 0:n], tap[0:m, ti, 0:1], None, ALU.mult, reads=[ps, tap], writes=[uk(ti)])
                    P.stt(u[0:m, ti, :], ps[0:m, 1:n + 1], tap[0:m, ti, 1:2], u[0:m, ti, :], ALU.mult, ALU.add,
                          reads=[ps, tap, uk(ti)], writes=[uk(ti)])
                    P.stt(u[0:m, ti, :], ps[0:m, 2:n + 2], tap[0:m, ti, 2:3], u[0:m, ti, :], ALU.mult, ALU.add,
                          reads=[ps, tap, uk(ti)], writes=[uk(ti)])
                th, sg0, sg1 = R["th"], R["sg0"], R["sg1"]
                P.act(th[:], u[:, 6, :], AF.Tanh, reads=[uk(6)], writes=[th])
                P.act(sg0[:], u[:, 8, :], AF.Sigmoid, reads=[uk(8)], writes=[sg0])
                P.act(sg1[0:32, :], u[0:32, 9, :], AF.Sigmoid, reads=[uk(9)], writes=[sg1])
                for ct in range(2):
                    ps = npb()
                    P.mm(ps[:, 0:n], g2a[:, ct * 128:(ct + 1) * 128], sg0[:], start=True, stop=False, reads=[g2a, sg0], writes=[ps])
                    P.mm(ps[:, 0:n], g2b[0:32, ct * 128:(ct + 1) * 128], sg1[0:32, :], start=False, stop=True, reads=[g2b, sg1], writes=[ps])
                    gt = gate_t[ct]
                    P.copy(gt[:], ps[:, 0:n], reads=[ps], writes=[gt], eng="act")
                    P.dma(rw_gate[ct, :, g0:g0 + n], gt[:], reads=[gt], q="pool")
                    P.dma(rw_vT[ct, :, g0:g0 + n], u[:, 4 + ct, :], reads=[uk(4 + ct)], q="pool")
                for j in range(2):
                    ps = npb()
                    for ct in range(2):
                        P.tr(ps[:, ct * 128:(ct + 1) * 128], u[:, 4 + ct, j * 128:(j + 1) * 128], ident[:], reads=[uk(4 + ct), ident], writes=[ps])
                    P.copy(vt[:, j, :], ps[:, 0:256], reads=[ps], writes=[vt], eng="act")
                P.dma(rw_v[g0:g0 + n, :].rearrange("(j p) c -> p j c", p=128), vt[:], reads=[vt], q="pool")
                def chain(d, ct, R, fmo, tmo, u=u, uk=uk, g0=g0, n=n):
                    if True:
                        ukk = uk(2 + ct); urk = uk(ct)
                        u_k = u[:, 2 + ct, :]; u_r = u[:, ct, :]
                        lw, asg, kkr, sq, nr, kk, b_, t_, kd = (R[k] for k in ("lw", "asg", "kkr", "sq", "nr", "kk", "b", "t", "kd"))
                        cs_, cl, clx, e1, e2, e3, bh, kh, tz = (R[k] for k in ("cs", "cl", "clx", "e1", "e2", "e3", "bh", "kh", "tz"))
                        ps = npb()
                        P.mm(ps[:, 0:n], w2s[d * 64:(d + 1) * 64, ct * 128:(ct + 1) * 128], th[d * 64:(d + 1) * 64, :], reads=[w2s, th], writes=[ps])
                        yield
                        P.act(lw[:], ps[:, 0:n], AF.Sigmoid, reads=[ps, vec], writes=[lw], bias=vec[:, 0, d, ct:ct + 1])
                        yield
                        P.ts(lw[:], lw[:], DECAY_C, None, ALU.mult, reads=[lw], writes=[lw])
                        yield
                        ps = npb()
                        P.mm(ps[:, 0:n], a2s[d * 64:(d + 1) * 64, ct * 128:(ct + 1) * 128], u[d * 64:(d + 1) * 64, 7, :], reads=[a2s, uk(7)], writes=[ps])
                        yield
                        P.act(asg[:], ps[:, 0:n], AF.Sigmoid, reads=[ps, vec], writes=[asg], bias=vec[:, 1, d, ct:ct + 1])
                        yield
                        P.ts(kkr[:], u_k, vec[:, 2, d, ct:ct + 1], None, ALU.mult, reads=[ukk, vec], writes=[kkr])
                        yield
                        P.tt(sq[:], kkr[:], kkr[:], ALU.mult, reads=[kkr], writes=[sq])
                        yield
                        ps = npb()
                        P.mm(ps[:, 0:n], bd64[:], sq[:], reads=[bd64, sq], writes=[ps])
                        yield
                        P.act(nr[:], ps[:, 0:n], AF.Sqrt, reads=[ps], writes=[nr])
                        yield
                        P.ts(nr[:], nr[:], 1e-12, None, ALU.max, reads=[nr], writes=[nr])
                        yield
                        P.recip(nr[:], nr[:], reads=[nr], writes=[nr])
                        yield
                        P.tt(kk[:], kkr[:], nr[:], ALU.mult, reads=[kkr, nr], writes=[kk])
                        yield
                        P.tt(b_[:], kk[:], asg[:], ALU.mult, reads=[kk, asg], writes=[b_])
                        yield
                        P.ts(t_[:], asg[:], -1.0, vec[:, 3, d, ct:ct + 1], ALU.add, ALU.mult, reads=[asg, vec], writes=[t_])
                        yield
                        P.stt(kd[:], t_[:], 1.0, u_k, ALU.add, ALU.mult, reads=[t_, ukk], writes=[kd])
                        yield
                        for j in range(2):
                            sl = slice(j * 128, (j + 1) * 128)
                            P.op("dve", lambda e, sl=sl: e.tensor_tensor_scan(cs_[:, sl], ones_f[:, 0:128], lw[:, sl], 0.0, ALU.mult, ALU.add),
                                 reads=[ones_f, lw], writes=[cs_])
                            yield
                        if d == 0:
                            clt = cs_
                        else:
                            for j in range(2):
                                sl = slice(j * 128, (j + 1) * 128)
                                P.ts(cl[:, sl], cs_[:, sl], -1.0, cs_[:, j * 128 + 127:j * 128 + 128], ALU.mult, ALU.add, reads=[cs_], writes=[cl])
                                yield
                            P.tt(cl[:], cl[:], lw[:], ALU.add, reads=[cl, lw], writes=[cl])
                            yield
                            clt = cl
                        P.tt(clx[:], clt[:], lw[:], ALU.subtract, reads=[clt, lw], writes=[clx])
                        yield
                        P.act(e1[:], clt[:], AF.Exp, reads=[clt], writes=[e1])
                        yield
                        P.act(e2[:], clt[:], AF.Exp, reads=[clt], writes=[e2], scale=-1.0)
                        yield
                        P.act(e3[:], clx[:], AF.Exp, reads=[clx], writes=[e3])
                        yield
                        for j in range(2):
                            cidx = g0 // 128 + j
                            col = j * 128 + (127 if d == 0 else 0)
                            P.copy(pcs[:, d, ct, cidx:cidx + 1], e1[:, col:col + 1], reads=[e1], writes=[pcs])
                            yield
                        P.stt(fmo[:, 0, ct, :], kk[:], -1.0, e3[:], ALU.mult, ALU.mult, reads=[kk, e3], writes=[fmo])
                        yield
                        P.tt(fmo[:, 1, ct, :], b_[:], e2[:], ALU.mult, reads=[b_, e2], writes=[fmo])
                        yield
                        P.tt(fmo[:, 2, ct, :], kd[:], e2[:], ALU.mult, reads=[kd, e2], writes=[fmo])
                        yield
                        P.tt(fmo[:, 3, ct, :], u_r, e1[:], ALU.mult, reads=[urk, e1], writes=[fmo])
                        yield
                        for j in range(2):
                            sl = slice(j * 128, (j + 1) * 128)
                            cidx = g0 // 128 + j
                            P.ts(bh[:, sl], fmo[:, 1, ct, sl], pcs[:, d, ct, cidx:cidx + 1], None, ALU.mult, reads=[fmo, pcs], writes=[bh])
                            yield
                            P.ts(kh[:, sl], fmo[:, 2, ct, sl], pcs[:, d, ct, cidx:cidx + 1], None, ALU.mult, reads=[fmo, pcs], writes=[kh])
                            yield
                        psT = npb()
                        for a, src in ((0, bh), (1, kh)):
                            for j in range(2):
                                P.tr(psT[:, (a * 2 + j) * 128:(a * 2 + j + 1) * 128], src[:, j * 128:(j + 1) * 128], ident[:],
                                     reads=[src, ident], writes=[psT])
                                yield
                        P.copy(tmo[:, :, :, ct * 128:(ct + 1) * 128], psT[:, :].rearrange("p (a j c) -> p j a c", a=2, j=2),
                               reads=[psT], writes=[tmo], eng="act")
                        yield
                        if d == 0:
                            P.stt(zz[:, ct, :], kd[:], vec[:, 4, d, ct:ct + 1], u_r, ALU.mult, ALU.mult, reads=[kd, vec, urk], writes=[zz])
                            yield
                        else:
                            P.stt(tz[:], kd[:], vec[:, 4, d, ct:ct + 1], u_r, ALU.mult, ALU.mult, reads=[kd, vec, urk], writes=[tz])
                            yield
                            P.tt(zz[:, ct, :], zz[:, ct, :], tz[:], ALU.add, reads=[zz, tz], writes=[zz])
                            yield
                for d in range(2):
                    fmo = fmout[d]; tmo = tmout[d]
                    gens = [chain(d, ct, Rs[ct], fmo, tmo) for ct in range(2)]
                    while gens:
                        alive = []
                        for g_ in gens:
                            try:
                                next(g_)
                                alive.append(g_)
                            except StopIteration:
                                pass
                        gens = alive
                    P.dma(rw_fm[d].rearrange("a c p t -> (a c) p t")[:, :, g0:g0 + n].rearrange("q p t -> p q t"),
                          fmo[:].rearrange("p a c t -> p (a c) t"), reads=[fmo], q="pool")
                    for a in range(2):
                        P.dma(rw_tm[d][a, g0:g0 + n, :].rearrange("(j p) c -> p j c", p=128), tmo[:, :, a, :], reads=[tmo], q="pool")
                P.dma(rw_zz[:, :, g0:g0 + n].rearrange("c p t -> p c t"), zz[:], reads=[zz], q="pool")

    def phase3(L, want_ctx):
        with P.phase():
            tri = P.sb([128, 4, 128])
            P.dma(tri[:], c_tri.rearrange("a p c -> p a c"), writes=[tri])
            LT, LE, GT, GE = (tri[:, i, :] for i in range(4))
            lnv = P.sb([128, 2, 2]); gne = P.sb([128, 1])
            P.dma(lnv[:], rw_ln[L], writes=[lnv])
            P.memset(gne[:], GN_EPS, writes=[gne])
            H = [P.sb([128, 2, 64]) for _ in range(2)]
            for d in range(2):
                P.memset(H[d][:].rearrange("p a b -> p (a b)"), 0.0, writes=[H[d]])
            yacc = P.sb([128, 2, T])
            fm = [P.sb([128, 4, 2, 128]) for _ in range(4)]
            tm = [P.sb([128, 2, 256]) for _ in range(4)]
            vv = [P.sb([128, 256]) for _ in range(4)]
            NB = 8
            W = {k: [P.sb([128, 128]) for _ in range(NB)] for k in ("N", "A", "Kt", "Rb", "Rk", "M", "N2", "A2", "Nd", "Ad", "MT")}
            W["Tt"] = [P.sb([128, 256]) for _ in range(NB)]
            blk = P.sb([128, 4, 128])
            P.dma(blk[:], c_blk.rearrange("a p c -> p a c"), writes=[blk])
            BD16, QS16, QS32, QS64 = (blk[:, i, :] for i in range(4))
            rhs_t = [P.sb([128, 64]) for _ in range(NB)]
            u_t = [P.sb([128, 64]) for _ in range(NB)]
            bank_pre = pb[0:2]; bank_sq = pb[2:4]; bank_ch = pb[5:7]; ybank = [pb[7], pb[4]]
            written = set()
            order = {0: list(range(NKT)), 1: [1, 0] + list(range(NKT - 1, 1, -1))}
            def unit(d, c, us, f_, t_, v_, ct, hh, ypair, mS, mA, mI, yc0):
                    h = ct * 2 + hh
                    p0 = hh * 64
                    at = f_[p0:p0 + 64, 0, ct, :]; bt = f_[p0:p0 + 64, 1, ct, :]
                    kt_ = f_[p0:p0 + 64, 2, ct, :]; rt = f_[p0:p0 + 64, 3, ct, :]
                    bh = t_[:, 0, h * 64:(h + 1) * 64]; kh = t_[:, 1, h * 64:(h + 1) * 64]
                    vh = v_[:, h * 64:(h + 1) * 64]
                    i_ = us
                    Nn, Aa, Kt, Rb, Rk, Mm = (W[k][i_ % NB] for k in ("N", "A", "Kt", "Rb", "Rk", "M"))
                    N2 = W["N2"][i_ % NB]; A2 = W["A2"][i_ % NB]
                    rh = rhs_t[i_ % NB]; uu = u_t[i_ % NB]
                    bp = nxt("bpre", bank_pre)
                    P.mm(bp[:, 0:128], bt, at, reads=[f_], writes=[bp])
                    P.mm(bp[:, 128:256], at, bt, reads=[f_], writes=[bp])
                    P.mm(bp[:, 256:384], kt_, at, reads=[f_], writes=[bp])
                    P.mm(bp[:, 384:512], bt, rt, reads=[f_], writes=[bp])
                    P.tt(Nn[:], bp[:, 0:128], mS, ALU.mult, reads=[bp, tri], writes=[Nn])
                    P.tt(Aa[:], bp[:, 128:256], mA, ALU.mult, reads=[bp, tri], writes=[Aa])
                    P.tt(Kt[:], bp[:, 256:384], mS, ALU.mult, reads=[bp, tri], writes=[Kt])
                    P.tt(Rb[:], bp[:, 384:512], mI, ALU.mult, reads=[bp, tri], writes=[Rb])
                    bp2 = nxt("bpre", bank_pre)
                    P.mm(bp2[:, 0:128], kt_, rt, reads=[f_], writes=[bp2])
                    P.tt(Rk[:], bp2[:, 0:128], mI, ALU.mult, reads=[bp2, tri], writes=[Rk])
                    Nd, Ad, MT, Tt = (W[k][i_ % NB] for k in ("Nd", "Ad", "MT", "Tt"))
                    P.tt(Nd[:], Nn[:], BD16, ALU.mult, reads=[Nn, blk], writes=[Nd], eng="pool")
                    P.tt(Ad[:], Aa[:], BD16, ALU.mult, reads=[Aa, blk], writes=[Ad], eng="pool")
                    P.tt(Mm[:], Nd[:], ident[:], ALU.add, reads=[Nd, ident], writes=[Mm], eng="pool")
                    P.tt(MT[:], Ad[:], ident[:], ALU.add, reads=[Ad, ident], writes=[MT], eng="pool")
                    yield
                    curN, curA, nxtN, nxtA = Nd, Ad, N2, A2
                    for lev in range(1, 4):
                        bs = nxt("bsq", bank_sq)
                        P.mm(bs[:, 0:128], curN[:], curA[:], reads=[curN, curA], writes=[bs])
                        P.mm(bs[:, 128:256], curA[:], curN[:], reads=[curN, curA], writes=[bs])
                        P.copy(nxtA[:], bs[:, 0:128], reads=[bs], writes=[nxtA], eng="act")
                        P.copy(nxtN[:], bs[:, 128:256], reads=[bs], writes=[nxtN], eng="act")
                        yield
                        bs2 = nxt("bsq", bank_sq)
                        P.mm(bs2[:, 0:128], nxtA[:], Mm[:], reads=[nxtA, Mm], writes=[bs2])
                        P.mm(bs2[:, 128:256], nxtN[:], MT[:], reads=[nxtN, MT], writes=[bs2])
                        P.tt(Mm[:], Mm[:], bs2[:, 0:128], ALU.add, reads=[Mm, bs2], writes=[Mm])
                        P.tt(MT[:], MT[:], bs2[:, 128:256], ALU.add, reads=[MT, bs2], writes=[MT])
                        curN, curA, nxtN, nxtA = nxtN, nxtA, curN, curA
                        yield
                    for qi, QS in enumerate((QS16, QS32, QS64)):
                        last = (qi == 2)
                        NQ, AQ = curN, curA
                        P.tt(AQ[:], Aa[:], QS, ALU.mult, reads=[Aa, blk], writes=[AQ], eng="pool")
                        if not last:
                            P.tt(NQ[:], Nn[:], QS, ALU.mult, reads=[Nn, blk], writes=[NQ], eng="pool")
                        bs = nxt("bsq", bank_sq)
                        P.mm(bs[:, 0:128], AQ[:], Mm[:], reads=[AQ, Mm], writes=[bs])
                        if not last:
                            P.mm(bs[:, 128:256], NQ[:], MT[:], reads=[NQ, MT], writes=[bs])
                        P.copy(Tt[:, 0:128], bs[:, 0:128], reads=[bs], writes=[Tt], eng="act")
                        if not last:
                            P.copy(Tt[:, 128:256], bs[:, 128:256], reads=[bs], writes=[Tt], eng="act")
                        yield
                        bs2 = nxt("bsq", bank_sq)
                        P.mm(bs2[:, 0:128], MT[:], Tt[:, 0:128], reads=[MT, Tt], writes=[bs2])
                        if not last:
                            P.mm(bs2[:, 128:256], Mm[:], Tt[:, 128:256], reads=[Mm, Tt], writes=[bs2])
                        P.tt(Mm[:], Mm[:], bs2[:, 0:128], ALU.add, reads=[Mm, bs2], writes=[Mm])
                        if not last:
                            P.tt(MT[:], MT[:], bs2[:, 128:256], ALU.add, reads=[MT, bs2], writes=[MT])
                        yield
                    Hh = H[d][p0:p0 + 64, ct, :]
                    bc = nxt("bch", bank_ch)
                    P.mm(bc[:, 0:64], at, Hh, start=True, stop=False, reads=[f_, H[d]], writes=[bc])
                    P.mm(bc[:, 0:64], Kt[:], vh, start=False, stop=True, reads=[Kt, v_], writes=[bc])
                    P.copy(rh[:], bc[:, 0:64], reads=[bc], writes=[rh], eng="act")
                    yield
                    P.mm(bc[:, 64:128], Mm[:], rh[:], reads=[Mm, rh], writes=[bc])
                    P.copy(uu[:], bc[:, 64:128], reads=[bc], writes=[uu], eng="act")
                    yield
                    P.mm(ypair[p0:p0 + 64, yc0:yc0 + 128], Hh, rt, start=True, stop=False, reads=[H[d], f_], writes=[ypair])
                    P.mm(ypair[p0:p0 + 64, yc0:yc0 + 128], uu[:], Rb[:], start=False, stop=False, reads=[uu, Rb], writes=[ypair])
                    P.mm(ypair[p0:p0 + 64, yc0:yc0 + 128], vh, Rk[:], start=False, stop=True, reads=[v_, Rk], writes=[ypair])
                    P.mm(bc[p0:p0 + 64, 128:192], bh, uu[:], start=True, stop=False, reads=[t_, uu], writes=[bc])
                    P.mm(bc[p0:p0 + 64, 128:192], kh, vh, start=False, stop=True, reads=[t_, v_], writes=[bc])
                    P.stt(Hh, Hh, pcs[p0:p0 + 64, d, ct, c:c + 1], bc[p0:p0 + 64, 128:192], ALU.mult, ALU.add,
                          reads=[H[d], pcs, bc], writes=[H[d]])
                    yield

            def drive(gens, lock):
                if not lock:
                    for g_ in gens:
                        for _ in g_:
                            pass
                    return
                while gens:
                    alive = []
                    for g_ in gens:
                        try:
                            next(g_)
                            alive.append(g_)
                        except StopIteration:
                            pass
                    gens = alive

            for step in range(NKT):
                gens = []
                cs_ = {}
                for d in range(2):
                    c = order[d][step]
                    cs_[d] = c
                    g = c * 128
                    f_ = nxt("fm3", fm); t_ = nxt("tm3", tm); v_ = nxt("vv3", vv)
                    P.dma(f_[:].rearrange("p a c t -> p (a c) t"),
                          rw_fm[d].rearrange("a c p t -> (a c) p t")[:, :, g:g + 128].rearrange("q p t -> p q t"), writes=[f_])
                    P.dma(t_[:], rw_tm[d][:, g:g + 128, :].rearrange("a t c -> t a c"), writes=[t_])
                    P.dma(v_[:], rw_v[g:g + 128, :], writes=[v_])
                    mS, mA, mI = (LT, GT, LE) if d == 0 else (GT, LT, GE)
                    for ct in range(2):
                        for hh in range(2):
                            gens.append(unit(d, c, d * 4 + ct * 2 + hh, f_, t_, v_, ct, hh, ybank[hh], mS, mA, mI, (d * 2 + ct) * 128))
                    if RW_LOCK < 8:
                        drive(gens, RW_LOCK > 1)
                        gens = []
                if gens:
                    drive(gens, True)
                for d in range(2):
                    c = cs_[d]
                    g = c * 128
                    for ct in range(2):
                        yc0 = (d * 2 + ct) * 128
                        first = (ct, c) not in written
                        written.add((ct, c))
                        for hh in range(2):
                            p0 = hh * 64
                            ysl = yacc[p0:p0 + 64, ct, g:g + 128]
                            yk_ = f"yacc{ct}_{hh}"
                            if first:
                                P.copy(ysl, ybank[hh][p0:p0 + 64, yc0:yc0 + 128], reads=[ybank[hh]], writes=[yk_], eng="act")
                            else:
                                P.tt(ysl, ysl, ybank[hh][p0:p0 + 64, yc0:yc0 + 128], ALU.add, reads=[yk_, ybank[hh]], writes=[yk_])
            if f"rw{L}" in dbg:
                dump(f"rwH0_{L}", H[0][:], [128, 2, 64], [H[0]])
                dump(f"rwH1_{L}", H[1][:], [128, 2, 64], [H[1]])
                dump(f"rwpcs_{L}", pcs[:], [128, 2, 2, NKT], [pcs])
                dump(f"rwyacc_{L}", yacc[:], [128, 2, T], [yacc])
                for d in range(2):
                    tdb = P.sb([128, 8, 512])
                    P.dma(tdb[:], rw_fm[d].rearrange("a c p t -> (a c) p t")[:, :, 0:512].rearrange("q p t -> p q t"), writes=[tdb])
                    dump(f"rwfm{d}_{L}", tdb[:], [128, 8, 512], [tdb])
            gt_ = [P.sb([128, 512]) for _ in range(2)]; vT_ = [P.sb([128, 512]) for _ in range(2)]; zz_ = [P.sb([128, 512]) for _ in range(2)]
            w1 = [P.sb([128, 512]) for _ in range(2)]; w2_ = [P.sb([128, 512]) for _ in range(2)]; w3 = [P.sb([128, 512]) for _ in range(2)]
            ob3 = [P.sb([128, 512], BF16) for _ in range(2)]
            for bi, (g0, n) in enumerate(BLOCKS):
                if g0 < T_CTX and not want_ctx:
                    continue
                for ct in range(2):
                    y = yacc[:, ct, g0:g0 + n]
                    g_ = nxt("gt_", gt_); v2 = nxt("vT_", vT_); z2 = nxt("zz_", zz_)
                    P.dma(g_[:, 0:n], rw_gate[ct, :, g0:g0 + n], writes=[g_])
                    P.dma(v2[:, 0:n], rw_vT[ct, :, g0:g0 + n], writes=[v2])
                    P.dma(z2[:, 0:n], rw_zz[ct, :, g0:g0 + n], writes=[z2])
                    ps = npb()
                    P.mm(ps[:, 0:n], bd64[:], y, reads=[bd64, f"yacc{ct}_0", f"yacc{ct}_1"], writes=[ps])
                    yc = nxt("w1", w1)
                    P.stt(yc[:, 0:n], ps[:, 0:n], -1.0 / 64, y, ALU.mult, ALU.add, reads=[ps, f"yacc{ct}_0", f"yacc{ct}_1"], writes=[yc])
                    sq = nxt("w2_", w2_)
                    P.tt(sq[:, 0:n], yc[:, 0:n], yc[:, 0:n], ALU.mult, reads=[yc], writes=[sq], eng="pool")
                    ps2 = npb()
                    P.mm(ps2[:, 0:n], bd64[:], sq[:, 0:n], reads=[bd64, sq], writes=[ps2])
                    P.act(sq[:, 0:n], ps2[:, 0:n], AF.Sqrt, reads=[ps2, gne], writes=[sq], scale=1.0 / 64, bias=gne[:, 0:1])
                    P.recip(sq[:, 0:n], sq[:, 0:n], reads=[sq], writes=[sq])
                    P.tt(yc[:, 0:n], yc[:, 0:n], sq[:, 0:n], ALU.mult, reads=[yc, sq], writes=[yc])
                    o1 = nxt("w3", w3)
                    P.act(o1[:, 0:n], yc[:, 0:n], AF.Identity, reads=[yc, lnv], writes=[o1], scale=lnv[:, 0, ct:ct + 1], bias=lnv[:, 1, ct:ct + 1])
                    ps3 = npb()
                    P.mm(ps3[:, 0:n], bd64[:], z2[:, 0:n], reads=[bd64, z2], writes=[ps3])
                    P.tt(v2[:, 0:n], ps3[:, 0:n], v2[:, 0:n], ALU.mult, reads=[ps3, v2], writes=[v2])
                    P.tt(o1[:, 0:n], o1[:, 0:n], v2[:, 0:n], ALU.add, reads=[o1, v2], writes=[o1], eng="pool")
                    o = nxt("ob3", ob3)
                    P.tt(o[:, 0:n], o1[:, 0:n], g_[:, 0:n], ALU.mult, reads=[o1, g_], writes=[o])
                    P.dma(mixT[6 + ct, :, g0:g0 + n], o[:, 0:n], reads=[o], q="pool")

    def phase4(L, want_ctx):
        with P.phase():
            T_ = {"big": [P.sb([128, 8, 512]) for _ in range(5)],
                  "tmp": [P.sb([128, 512]) for _ in range(4)],
                  "rs": [P.sb([128, 512]) for _ in range(2)]}
            wo = P.sb([128, 8, 1024], BF16)
            mb = [P.sb([128, 8, 512], BF16) for _ in range(2)]
            hb = [P.sb([128, 8, 512], BF16) for _ in range(2)]
            load_w_bf16(wo[:], wo, w_out[L], D, D, T_)
            set_norm_params(2)
            for bi, (g0, n) in enumerate(BLOCKS):
                if g0 < T_CTX and not want_ctx:
                    continue
                s = 1 if g0 < T_CTX else 0
                m_ = nxt("mb", mb)
                P.dma(m_[:, :, 0:n], mixT[:, :, g0:g0 + n].rearrange("a p t -> p a t"), writes=[m_])
                xb = nxt("big", T_["big"])
                P.dma(xb[:, :, 0:n], xT[:, :, g0:g0 + n].rearrange("a p t -> p a t"), writes=[xb])
                x1 = nxt("big", T_["big"])
                for ft in range(8):
                    ps = npb()
                    for kt in range(8):
                        P.mm(ps[:, 0:n], wo[:, kt, ft * 128:(ft + 1) * 128], m_[:, kt, 0:n], start=(kt == 0), stop=(kt == 7),
                             reads=[wo, m_], writes=[ps])
                    P.stt(x1[:, ft, 0:n], ps[:, 0:n], modT[:, 16 + ft, s:s + 1], xb[:, ft, 0:n], ALU.mult, ALU.add,
                          reads=[ps, modT, xb], writes=[x1])
                P.dma(xT[:, :, g0:g0 + n].rearrange("a p t -> p a t"), x1[:, :, 0:n], reads=[x1], q="pool")
                h_ = nxt("hb", hb)
                norm_block(x1, n, (lambda ft, h_=h_, n=n: (h_[:, ft, 0:n], [h_])),
                           (lambda ft, s=s: gs[:, ft, s:s + 1]), (lambda ft, s=s: sh[:, ft, s:s + 1]), T_)
                P.dma(h2T[:, :, g0:g0 + n].rearrange("a p t -> p a t"), h_[:, :, 0:n], reads=[h_], q="pool")

    def phase5(L, want_ctx):
        for hf in range(2):
            with P.phase():
                T_ = {"big": [P.sb([128, 8, 512]) for _ in range(4)]}
                f1 = P.sb([128, 8, 2048], BF16)
                f2 = P.sb([128, 16, 1024], BF16)
                hb = [P.sb([128, 8, 512], BF16) for _ in range(2)]
                ab = [P.sb([128, 16, 512], BF16) for _ in range(2)]
                rt_ = [P.sb([128, 512], BF16) for _ in range(3)]
                load_w_bf16(f1[:], f1, w_fc1[L][:, hf * 2048:(hf + 1) * 2048], D, 2048, T_)
                load_w_bf16(f2[:], f2, w_fc2[L][hf * 2048:(hf + 1) * 2048, :], 2048, D, T_)
                for bi, (g0, n) in enumerate(BLOCKS):
                    if g0 < T_CTX and not want_ctx:
                        continue
                    s = 1 if g0 < T_CTX else 0
                    h_ = nxt("hb5", hb)
                    P.dma(h_[:, :, 0:n], h2T[:, :, g0:g0 + n].rearrange("a p t -> p a t"), writes=[h_])
                    xb = nxt("big", T_["big"])
                    P.dma(xb[:, :, 0:n], xT[:, :, g0:g0 + n].rearrange("a p t -> p a t"), writes=[xb])
                    a_ = nxt("ab", ab)
                    for ht in range(16):
                        ps = npb()
                        for kt in range(8):
                            P.mm(ps[:, 0:n], f1[:, kt, ht * 128:(ht + 1) * 128], h_[:, kt, 0:n], start=(kt == 0), stop=(kt == 7),
                                 reads=[f1, h_], writes=[ps])
                        r_ = nxt("rt", rt_)
                        P.act(r_[:, 0:n], ps[:, 0:n], AF.Relu, reads=[ps], writes=[r_])
                        P.tt(a_[:, ht, 0:n], r_[:, 0:n], r_[:, 0:n], ALU.mult, reads=[r_], writes=[a_], eng=("dve" if ht % 2 else "pool"))
                    for ft in range(8):
                        ps = npb()
                        for ht in range(16):
                            P.mm(ps[:, 0:n], f2[:, ht, ft * 128:(ft + 1) * 128], a_[:, ht, 0:n], start=(ht == 0), stop=(ht == 15),
                                 reads=[f2, a_], writes=[ps])
                        P.stt(xb[:, ft, 0:n], ps[:, 0:n], modT[:, 40 + ft, s:s + 1], xb[:, ft, 0:n], ALU.mult, ALU.add,
                              reads=[ps, modT, xb], writes=[xb])
                    P.dma(xT[:, :, g0:g0 + n].rearrange("a p t -> p a t"), xb[:, :, 0:n], reads=[xb], q="pool")

    def stage_final():
        with P.phase():
            T_ = {"big": [P.sb([128, 8, 512]) for _ in range(5)],
                  "tmp": [P.sb([128, 512]) for _ in range(4)],
                  "rs": [P.sb([128, 512]) for _ in range(2)]}
            yo = [P.sb([128, 4, 1024]) for _ in range(2)]
            for bi, (g0, n) in enumerate(BLOCKS):
                if g0 < T_CTX:
                    continue
                xb = nxt("big", T_["big"])
                P.dma(xb[:, :, 0:n], xT[:, :, g0:g0 + n].rearrange("a p t -> p a t"), writes=[xb])
                yb = nxt("big", T_["big"])
                norm_block(xb, n, (lambda ft, yb=yb, n=n: (yb[:, ft, 0:n], [yb])), (lambda ft: fg[:, ft:ft + 1]), (lambda ft: None), T_)
                y_ = nxt("yo", yo)
                for a in range(4):
                    for f2_ in range(2):
                        ps = npb()
                        for q4 in range(4):
                            ft = f2_ * 4 + q4
                            P.tr(ps[:, q4 * 128:(q4 + 1) * 128], yb[:, ft, a * 128:(a + 1) * 128], ident[:], reads=[yb, ident], writes=[ps])
                        P.copy(y_[:, a, f2_ * 512:(f2_ + 1) * 512], ps[:, :], reads=[ps], writes=[y_], eng=("act" if f2_ else "dve"))
                t0 = g0 - T_CTX
                P.dma(y_out[t0:t0 + n, :].rearrange("(a p) f -> p a f", p=128), y_[:], reads=[y_], q="pool")

    stage_A()
    for L in range(nl):
        want_ctx = (L < DEPTH - 1) if force_ctx is None else force_ctx
        phase1(L)
        phase2(L, want_ctx)
        if "rw" in mixers:
            phase3(L, want_ctx)
        if "mixT" in dbg or f"mixT{L}" in dbg:
            with P.phase():
                t_ = P.sb([128, 8, T], BF16)
                P.dma(t_[:], mixT.rearrange("a p t -> p a t"), writes=[t_])
                dump("mixT" if "mixT" in dbg else f"mixT{L}", t_[:], [128, 8, T], [t_], BF16)
        if f"xT{L}" in dbg:
            with P.phase():
                t_ = P.sb([128, 8, T])
                P.dma(t_[:], xT.rearrange("a p t -> p a t"), writes=[t_])
                dump(f"xT{L}", t_[:], [128, 8, T], [t_])
        if "rw" not in mixers:
            with P.phase():
                z_ = P.sb([128, 2, T], BF16)
                P.memset(z_[:].rearrange("p a t -> p (a t)"), 0.0, writes=[z_])
                P.dma(mixT[6:8].rearrange("a p t -> p a t"), z_[:], reads=[z_], q="pool")
        phase4(L, want_ctx)
        if do_mlp:
            phase5(L, want_ctx)
    if "xT" in dbg:
        with P.phase():
            t_ = P.sb([128, 8, T])
            P.dma(t_[:], xT.rearrange("a p t -> p a t"), writes=[t_])
            dump("xT", t_[:], [128, 8, T], [t_])
    stage_final()
    P.emit()
    P.close()
    return nc


_CONSTS = None


def consts():
    global _CONSTS
    if _CONSTS is None:
        c64, s64 = rope_tables(64)
        c32, s32 = rope_tables(32)
        c96 = np.ones((96, T), np.float32); s96 = np.zeros((96, T), np.float32)
        c96[64:] = c32; s96[64:] = s32
        bd = np.zeros((128, 128), np.float32); bd[:64, :64] = 1; bd[64:, 64:] = 1
        _CONSTS = {
            "c_ident": np.eye(128, dtype=np.float32), "c_bd64": bd,
            "c_rt128": np.ascontiguousarray(np.kron(np.eye(2, dtype=np.float32), rot_lhsT(64))),
            "c_rt96": rot_lhsT(32, off=64, size=96),
            "c_cos128": np.ascontiguousarray(np.tile(c64, (2, 1))), "c_sin128": np.ascontiguousarray(np.tile(s64, (2, 1))),
            "c_cos96": c96, "c_sin96": s96,
            "c_tri": np.stack([np.triu(np.ones((128, 128), np.float32), 1), np.triu(np.ones((128, 128), np.float32), 0),
                               np.tril(np.ones((128, 128), np.float32), -1), np.tril(np.ones((128, 128), np.float32), 0)], 0),
            "c_blk": (lambda bd: np.stack([bd(16), bd(32) - bd(16), bd(64) - bd(32), 1.0 - bd(64)], 0).astype(np.float32))(
                lambda b: np.kron(np.eye(128 // b), np.ones((b, b)))),
            "na_masks": np.ascontiguousarray(NA_MASKS).astype(ml_dtypes.bfloat16),
        }
    return _CONSTS


def prep_shared(inp):
    f = np.float32
    L = DEPTH
    w_in = np.asarray(inp["w_in"], f)
    kr96 = np.zeros((L, D, 96), f)
    kr96[:, :, 64:] = w_in[:, :, MLA_C0 + 384:MLA_C0 + 416]
    wukv = np.asarray(inp["mla_w_ukv"], f).reshape(L, 128, 4, 2, 64)
    wukv = np.ascontiguousarray(np.concatenate([wukv[:, :, :, 0, :].reshape(L, 128, 256), wukv[:, :, :, 1, :].reshape(L, 128, 256)], -1))
    sh = dict(consts())
    sh.update({
        "w_mod": np.asarray(inp["w_mod"], f), "b_modT": featT(inp["b_mod"]), "n1g": featT(inp["norm1_g"]),
        "n2g": featT(inp["norm2_g"]), "fng": featT(inp["final_norm_g"]), "w_in": w_in, "w_kr96": kr96,
        "gqa_qn": np.tile(np.asarray(inp["gqa_q_norm"], f), (1, 2))[:, :, None].copy(), "gqa_kn": np.tile(np.asarray(inp["gqa_k_norm"], f), (1, 2))[:, :, None].copy(),
        "mla_qn": featT(inp["mla_q_norm"]), "mla_kvn": featT(inp["mla_kv_norm"]),
        "w_uq": np.asarray(inp["mla_w_uq"], f), "w_ukv": wukv,
        "na_tb": na_bias_table(np.asarray(inp["na_rpb"], f)), "na_rpbf": np.asarray(inp["na_rpb"], f).reshape(L, 4, 1, 465),
        "rw_tap": np.ascontiguousarray(np.pad(np.asarray(inp["rwkv_shift"], f), ((0, 0), (0, 0), (0, 96))).reshape(L, 3, 10, 128).transpose(0, 3, 2, 1)),
        "rw_vec": np.ascontiguousarray(np.stack([np.asarray(inp[k], f).reshape(L, 2, 2, 128) for k in
                                                 ("rwkv_w0", "rwkv_a0", "rwkv_k_k", "rwkv_k_a", "rwkv_r_k")], 1).transpose(0, 4, 1, 2, 3)),
        "rw_w2": np.asarray(inp["rwkv_w2"], f).reshape(L, 128, 256), "rw_a2": np.asarray(inp["rwkv_a2"], f).reshape(L, 128, 256),
        "rw_g2": np.asarray(inp["rwkv_g2"], f),
        "rw_ln": np.ascontiguousarray(np.stack([np.asarray(inp[k], f).reshape(L, 2, 128) for k in ("rwkv_lnx_w", "rwkv_lnx_b")], 1).transpose(0, 3, 1, 2)),
        "w_out": np.asarray(inp["w_out"], f), "w_fc1": np.asarray(inp["w_fc1"], f), "w_fc2": np.asarray(inp["w_fc2"], f),
    })
    return sh


def prep_core(inp, b, shared):
    m = dict(shared)
    m["x"] = np.ascontiguousarray(inp["x"][b], np.float32)
    m["ctx"] = np.ascontiguousarray(inp["ctx"][b], np.float32)
    cv = np.stack([featT(inp["c"][b]), featT(inp["c_ctx"])], -1).astype(np.float32)
    m["cvec"] = np.ascontiguousarray(cv)
    return m


def kernel(**inputs):
    nc = build()
    shared = prep_shared(inputs)
    in_maps = [prep_core(inputs, b, shared) for b in range(8)]
    res = run_bass_kernel_spmd(nc, in_maps, core_ids=list(range(8)))
    return np.stack([r["y"] for r in res.results], 0).astype(np.float32)
```
